# Optimizing a Trainium2 kernel written in Bass

```python
import jax
import jax.numpy as jnp
from jax import lax
import numpy as np

D_MODEL = 4096
BATCH = 8
SEQ = 2048
DEPTH = 4

GRID_W = 64
CTX_LEN = 256
N_MIXERS = 3
N_MOD = 9
ADA_RANK = D_MODEL // 16
FFN_DIM = (5 * D_MODEL) // 4
RMS_EPS = 1e-6
HGRN_EXPAND = 128
HGRN_HEADS = D_MODEL // HGRN_EXPAND
HGRN_DK = HGRN_EXPAND
HGRN_DV = D_MODEL // HGRN_HEADS
HGRN_CHUNK = 32
CONV_WIDTH = 3
NA_DH = 128
NA_HEADS = D_MODEL // NA_DH
NA_KH = 8
NA_KW = 16
NA_QB = NA_KW
NA_SLAB = 2 * NA_KW
N_HGRN_LAYERS = len(range(0, DEPTH, N_MIXERS))
N_CONV_LAYERS = len(range(1, DEPTH, N_MIXERS))
N_NA_LAYERS = len(range(2, DEPTH, N_MIXERS))

kernel_name = 'hybrid_hgrn2_shortconv_natten_macaron_dit'


def rms_norm(t, w):
    t32 = t.astype(jnp.float32)
    y = t32 * lax.rsqrt(jnp.mean(t32 * t32, axis=-1, keepdims=True) + RMS_EPS) * w.astype(jnp.float32)
    return y.astype(t.dtype)


def modulate(t, w, shift, scale):
    return rms_norm(t, w) * (1 + scale) + shift


def adaln_mods(cond, down, up, bias):
    m = (jax.nn.silu(cond) @ down) @ up + bias
    return jnp.split(m, N_MOD, axis=-1)


def swiglu(t, wi, wo):
    a, b = jnp.split(t @ wi, 2, axis=-1)
    return (jax.nn.silu(a) * b) @ wo


def hgrn_lower_bounds(lb_param):
    p = jax.nn.softmax(lb_param.astype(jnp.float32), axis=0)
    return jnp.cumsum(p, axis=0) - p[0]


def _heads(t, n_heads):
    b, t_len, _ = t.shape
    return t.reshape(b, t_len, n_heads, -1).transpose(0, 2, 1, 3)


def _flip_time(a, direction):
    return a[:, :, ::-1] if direction == 1 else a


def gla_chunk_scan(q, k, v, log_f, s0, with_out):
    b_sz, n_h, t_len, d_k = q.shape
    d_v = v.shape[-1]
    n_chunks = t_len // HGRN_CHUNK

    def chunks(a):
        return a.reshape(b_sz, n_h, n_chunks, HGRN_CHUNK, a.shape[-1]).transpose(2, 0, 1, 3, 4)

    tri = jnp.tril(jnp.ones((HGRN_CHUNK, HGRN_CHUNK), dtype=bool))[:, :, None]

    def step(state, blk):
        qc, kc, vc, gc = blk
        cum = jnp.cumsum(gc, axis=2)
        cum_end = cum[:, :, -1]
        k_to_end = kc * jnp.exp(cum_end[:, :, None] - cum)
        new_state = jnp.exp(cum_end)[..., None] * state + jnp.einsum('bhck,bhcv->bhkv', k_to_end, vc)
        if not with_out:
            return new_state, None
        o_prev = jnp.einsum('bhck,bhkv->bhcv', qc * jnp.exp(cum), state)
        rel = jnp.where(tri, cum[:, :, :, None] - cum[:, :, None], -jnp.inf)
        scores = jnp.einsum('bhtk,bhtsk,bhsk->bhts', qc, jnp.exp(rel), kc)
        return new_state, o_prev + jnp.einsum('bhts,bhsv->bhtv', scores, vc)

    final, o = lax.scan(step, s0, (chunks(q), chunks(k), chunks(v), chunks(log_f)))
    if with_out:
        o = o.transpose(1, 2, 0, 3, 4).reshape(b_sz, n_h, t_len, d_v)
    return o, final


def hgrn2_mixer(xin, hin, w_in, w_out, lb, gnorm_w, need_ctx):
    f32 = jnp.float32

    def project(t):
        q, z_fw, z_bw, v, g = jnp.split(t @ w_in, 5, axis=-1)
        q = _heads(jax.nn.silu(q.astype(f32)), HGRN_HEADS) * (HGRN_DK ** -0.5)
        v = _heads(v.astype(f32), HGRN_HEADS)
        kf = []
        for d, z in enumerate((z_fw, z_bw)):
            lbd = lb[d].reshape(HGRN_HEADS, 1, HGRN_DK)
            z = _heads(z.astype(f32), HGRN_HEADS)
            log_f = jnp.logaddexp(jnp.log(lbd), jnp.log1p(-lbd) + jax.nn.log_sigmoid(z))
            kf.append(((1.0 - lbd) * jax.nn.sigmoid(-z), log_f))
        return q, v, kf, g

    qx, vx, kfx, gx = project(xin)
    qc, vc, kfc, gc = project(hin)
    s_zero = jnp.zeros((xin.shape[0], HGRN_HEADS, HGRN_DK, HGRN_DV), f32)
    o_x = 0.0
    o_c = 0.0
    for d in range(2):
        (kx, fx), (kc, fc) = kfx[d], kfc[d]
        oc_d, s_ctx = gla_chunk_scan(_flip_time(qc, d), _flip_time(kc, d), _flip_time(vc, d),
                                     _flip_time(fc, d), s_zero, need_ctx)
        ox_d, _ = gla_chunk_scan(_flip_time(qx, d), _flip_time(kx, d), _flip_time(vx, d),
                                 _flip_time(fx, d), s_ctx, True)
        o_x = o_x + _flip_time(ox_d, d)
        if need_ctx:
            o_c = o_c + _flip_time(oc_d, d)

    def readout(o, g):
        b_sz, n_h, t_len, d_v = o.shape
        o = rms_norm(o, gnorm_w).transpose(0, 2, 1, 3).reshape(b_sz, t_len, n_h * d_v)
        return (o * jax.nn.silu(g.astype(f32))).astype(g.dtype) @ w_out

    y_h = readout(o_c, gc) if need_ctx else None
    return readout(o_x, gx), y_h


def short_conv_mixer(xin, hin, w_in, conv_w, w_out, need_ctx):
    pad = CONV_WIDTH // 2

    def run(t):
        b_gate, c_gate, u = jnp.split(t @ w_in, 3, axis=-1)
        z = c_gate * u
        t_len = z.shape[1]
        zp = jnp.pad(z, ((0, 0), (pad, pad), (0, 0)))
        conv = conv_w[0] * zp[:, 0:t_len]
        for i in range(1, CONV_WIDTH):
            conv = conv + conv_w[i] * zp[:, i:i + t_len]
        return (b_gate * conv) @ w_out

    y_h = run(hin) if need_ctx else None
    return run(xin), y_h


def _na_column_tables():
    n_cb = GRID_W // NA_QB
    cb = np.arange(n_cb)
    slab_start = np.clip(cb * NA_QB - NA_KW // 2, 0, GRID_W - NA_SLAB)
    slab_cols = slab_start[:, None] + np.arange(NA_SLAB)
    q_cols = cb[:, None] * NA_QB + np.arange(NA_QB)
    win_start = np.clip(q_cols - NA_KW // 2, 0, GRID_W - NA_KW)
    key_cols = slab_cols[:, None, :]
    valid = (key_cols >= win_start[..., None]) & (key_cols < win_start[..., None] + NA_KW)
    bias_idx = np.clip(key_cols - q_cols[..., None] + NA_KW - 1, 0, 2 * NA_KW - 2)
    return slab_cols, valid, bias_idx


def neighborhood_attn_mixer(xin, hin, w_qkv, w_out, q_norm_w, k_norm_w, rpb, need_ctx):
    b_sz, t_len, d_model = xin.shape
    rows = t_len // GRID_W
    kh = min(NA_KH, rows)
    n_cb = GRID_W // NA_QB
    n_lat = kh * NA_SLAB
    f32 = jnp.float32

    def project(t):
        q, k, v = jnp.split(t @ w_qkv, 3, axis=-1)
        shp = t.shape[:2] + (NA_HEADS, NA_DH)
        q = rms_norm(q.reshape(shp), q_norm_w) * (NA_DH ** -0.5)
        k = rms_norm(k.reshape(shp), k_norm_w)
        return q, k, v.reshape(shp)

    q, k, v = project(xin)
    qc, kc, vc = project(hin)
    grid = (b_sz, rows, GRID_W, NA_HEADS, NA_DH)
    qg, kg, vg = q.reshape(grid), k.reshape(grid), v.reshape(grid)
    slab_cols, valid, bias_idx = _na_column_tables()
    rpb_cols = rpb[:, :, bias_idx]

    def row_block(r):
        r0 = jnp.clip(r - kh // 2, 0, rows - kh)
        k_blk = lax.dynamic_slice_in_dim(kg, r0, kh, axis=1)[:, :, slab_cols]
        v_blk = lax.dynamic_slice_in_dim(vg, r0, kh, axis=1)[:, :, slab_cols]
        q_row = lax.dynamic_index_in_dim(qg, r, axis=1, keepdims=False).reshape(
            b_sz, n_cb, NA_QB, NA_HEADS, NA_DH)
        s_lat = jnp.einsum('bnqhd,binjhd->bhnqij', q_row, k_blk).astype(f32)
        row_bias = rpb_cols[:, r0 + jnp.arange(kh) - r + NA_KH - 1]
        s_lat = s_lat + row_bias.transpose(0, 2, 3, 1, 4).astype(f32)
        s_lat = jnp.where(valid[:, :, None, :], s_lat, -jnp.inf)
        s_ctx = jnp.einsum('bnqhd,blhd->bhnql', q_row, kc).astype(f32)
        logits = jnp.concatenate([s_lat.reshape(b_sz, NA_HEADS, n_cb, NA_QB, n_lat), s_ctx], axis=-1)
        p = jax.nn.softmax(logits, axis=-1).astype(v.dtype)
        p_lat = p[..., :n_lat].reshape(b_sz, NA_HEADS, n_cb, NA_QB, kh, NA_SLAB)
        o = (jnp.einsum('bhnqij,binjhd->bnqhd', p_lat, v_blk)
             + jnp.einsum('bhnql,blhd->bnqhd', p[..., n_lat:], vc))
        return o.reshape(b_sz, GRID_W, NA_HEADS, NA_DH)

    o = lax.map(row_block, jnp.arange(rows))
    y_x = o.transpose(1, 0, 2, 3, 4).reshape(b_sz, t_len, d_model) @ w_out
    y_h = None
    if need_ctx:
        s = jnp.einsum('blhd,bmhd->bhlm', qc, kc).astype(f32)
        p = jax.nn.softmax(s, axis=-1).astype(vc.dtype)
        y_h = jnp.einsum('bhlm,bmhd->blhd', p, vc).reshape(b_sz, -1, d_model) @ w_out
    return y_x, y_h


def setup_inputs(seed: int = 0) -> dict:
    key = jax.random.key(seed)
    ks = jax.random.split(key, 24)
    f32 = jnp.float32
    d = D_MODEL

    def nrm(k, shape):
        return jax.random.normal(k, shape, f32)

    def w(k, shape, fan_in):
        return nrm(k, shape) * (fan_in ** -0.5)

    return {
        'x': nrm(ks[0], (BATCH, SEQ, d)),
        'c': nrm(ks[1], (BATCH, d)),
        'ctx': nrm(ks[2], (BATCH, CTX_LEN, d)),
        'c_ctx': nrm(ks[3], (d,)),
        'ada_down': w(ks[4], (DEPTH, d, ADA_RANK), d),
        'ada_up': w(ks[5], (DEPTH, ADA_RANK, N_MOD * d), ADA_RANK),
        'ada_bias': 0.02 * nrm(ks[6], (DEPTH, N_MOD * d)),
        'norm_w': 1.0 + 0.02 * nrm(ks[7], (DEPTH, 3, d)),
        'ffn_wi': w(ks[8], (DEPTH, 2, d, 2 * FFN_DIM), d),
        'ffn_wo': w(ks[9], (DEPTH, 2, FFN_DIM, d), FFN_DIM),
        'hgrn_w_in': w(ks[10], (N_HGRN_LAYERS, d, 5 * d), d),
        'hgrn_w_out': w(ks[11], (N_HGRN_LAYERS, d, d), d),
        'hgrn_lb': 0.5 * nrm(ks[12], (DEPTH, 2, d)),
        'hgrn_gnorm': 1.0 + 0.02 * nrm(ks[13], (N_HGRN_LAYERS, HGRN_DV)),
        'sc_w_in': w(ks[14], (N_CONV_LAYERS, d, 3 * d), d),
        'sc_conv': w(ks[15], (N_CONV_LAYERS, CONV_WIDTH, d), CONV_WIDTH),
        'sc_w_out': w(ks[16], (N_CONV_LAYERS, d, d), d),
        'na_w_qkv': w(ks[17], (N_NA_LAYERS, d, 3 * d), d),
        'na_w_out': w(ks[18], (N_NA_LAYERS, d, d), d),
        'na_q_norm': 1.0 + 0.02 * nrm(ks[19], (N_NA_LAYERS, NA_DH)),
        'na_k_norm': 1.0 + 0.02 * nrm(ks[20], (N_NA_LAYERS, NA_DH)),
        'na_rpb': 0.02 * nrm(ks[21], (N_NA_LAYERS, NA_HEADS, 2 * NA_KH - 1, 2 * NA_KW - 1)),
    }


def reference(x, c, ctx, c_ctx, ada_down, ada_up, ada_bias, norm_w, ffn_wi, ffn_wo,
              hgrn_w_in, hgrn_w_out, hgrn_lb, hgrn_gnorm, sc_w_in, sc_conv, sc_w_out,
              na_w_qkv, na_w_out, na_q_norm, na_k_norm, na_rpb):
    lower_bounds = hgrn_lower_bounds(hgrn_lb)
    h = ctx
    for layer in range(DEPTH):
        kind = layer % N_MIXERS
        j = layer // N_MIXERS
        update_ctx = layer < DEPTH - 1
        ctx_needed = update_ctx or kind != 1
        wi_a, wo_a = ffn_wi[layer, 0], ffn_wo[layer, 0]
        wi_b, wo_b = ffn_wi[layer, 1], ffn_wo[layer, 1]
        mx = [m[:, None, :] for m in adaln_mods(c, ada_down[layer], ada_up[layer], ada_bias[layer])]
        x = x + 0.5 * mx[2] * swiglu(modulate(x, norm_w[layer, 0], mx[0], mx[1]), wi_a, wo_a)
        xin = modulate(x, norm_w[layer, 1], mx[3], mx[4])
        hin = None
        if ctx_needed:
            mc = adaln_mods(c_ctx, ada_down[layer], ada_up[layer], ada_bias[layer])
            h = h + 0.5 * mc[2] * swiglu(modulate(h, norm_w[layer, 0], mc[0], mc[1]), wi_a, wo_a)
            hin = modulate(h, norm_w[layer, 1], mc[3], mc[4])
        if kind == 0:
            y_x, y_h = hgrn2_mixer(xin, hin, hgrn_w_in[j], hgrn_w_out[j], lower_bounds[layer],
                                   hgrn_gnorm[j], update_ctx)
        elif kind == 1:
            y_x, y_h = short_conv_mixer(xin, hin, sc_w_in[j], sc_conv[j], sc_w_out[j], update_ctx)
        else:
            y_x, y_h = neighborhood_attn_mixer(xin, hin, na_w_qkv[j], na_w_out[j], na_q_norm[j],
                                               na_k_norm[j], na_rpb[j], update_ctx)
        x = x + mx[5] * y_x
        x = x + 0.5 * mx[8] * swiglu(modulate(x, norm_w[layer, 2], mx[6], mx[7]), wi_b, wo_b)
        if update_ctx:
            h = h + mc[5] * y_h
            h = h + 0.5 * mc[8] * swiglu(modulate(h, norm_w[layer, 2], mc[6], mc[7]), wi_b, wo_b)
    return x
```

```python
import contextlib
import numpy as np
import concourse.bass as bass
import concourse.mybir as mybir
from concourse.bass_utils import run_bass_kernel_spmd

F32 = mybir.dt.float32
BF16 = mybir.dt.bfloat16
AF = mybir.ActivationFunctionType
ALU = mybir.AluOpType
AX = mybir.AxisListType

D = 4096
NCH = 32
TL = 2048
TC = 256
T = TL + TC
FF = 5120
DEPTH = 4
EPS = 1e-6
GROUPS = [(0, 512), (512, 512), (1024, 512), (1536, 512), (2048, 256)]
N_CORES = 8


class Res:
    __slots__ = ("w", "r", "name")

    def __init__(self, name=""):
        self.w = None
        self.r = {}
        self.name = name


class Sch:
    NDQ = 8

    def __init__(self, nc, stack):
        self.nc = nc
        self.E = {"pe": nc.tensor, "act": nc.scalar, "dve": nc.vector, "pool": nc.gpsimd, "sp": nc.sync}
        self.csem = {}
        self.ccnt = {}
        self.semobj = {}
        for e in ("pe", "act", "dve", "pool"):
            self.csem[e] = stack.enter_context(nc.semaphore("c_" + e))
            self.ccnt[e] = 0
            self.semobj["c_" + e] = self.csem[e]
        self.dsem = {}
        self.dcnt = {}
        self.dlast = {}
        for q in ("sp", "pool", "act"):
            self.dsem[q] = [stack.enter_context(nc.semaphore("d_%s%d" % (q, i))) for i in range(self.NDQ)]
            self.dcnt[q] = 0
            self.dlast[q] = [None] * self.NDQ
            for i in range(self.NDQ):
                self.semobj["d_%s%d" % (q, i)] = self.dsem[q][i]
        self.seen = {e: {} for e in self.E}

    def _wait(self, eng, ev):
        key, val = ev
        s = self.seen[eng]
        if s.get(key, 0) >= val:
            return
        s[key] = val
        self.E[eng].wait_ge(self.semobj[key], val)

    def _deps(self, eng, reads, writes):
        skip = "c_pe" if eng == "pe" else None
        for r in reads:
            if r.w is not None and r.w[0] != skip:
                self._wait(eng, r.w)
        for w in writes:
            if w.w is not None and w.w[0] != skip:
                self._wait(eng, w.w)
            for k, v in w.r.items():
                if k != skip:
                    self._wait(eng, (k, v))

    def _commit(self, ev, reads, writes):
        k, v = ev
        for r in reads:
            if r.r.get(k, 0) < v:
                r.r[k] = v
        for w in writes:
            w.w = ev
            w.r = {}

    def op(self, eng, fn, reads=(), writes=()):
        self._deps(eng, reads, writes)
        ins = fn(self.E[eng])
        self.ccnt[eng] += 1
        ins.then_inc(self.csem[eng], 1)
        ev = ("c_" + eng, self.ccnt[eng])
        self._commit(ev, reads, writes)
        return ev

    def dma(self, q, out, in_, reads=(), writes=(), **kw):
        i = self.dcnt[q] % self.NDQ
        prev = self.dlast[q][i]
        if prev is not None:
            self._wait(q, prev)
        self._deps(q, reads, writes)
        ins = self.E[q].dma_start(out=out, in_=in_, **kw)
        self.dcnt[q] += 1
        key = "d_%s%d" % (q, i)
        val = 16 * ((self.dcnt[q] - 1) // self.NDQ + 1)
        ins.then_inc(self.dsem[q][i], 16)
        ev = (key, val)
        self.dlast[q][i] = ev
        self._commit(ev, reads, writes)
        return ev

    def barrier(self, engines=("pe", "act", "dve", "pool", "sp")):
        evs = []
        for e in ("pe", "act", "dve", "pool"):
            if self.ccnt[e]:
                evs.append(("c_" + e, self.ccnt[e]))
        for q in ("sp", "pool", "act"):
            for ev in self.dlast[q]:
                if ev is not None:
                    evs.append(ev)
        for eng in engines:
            for ev in evs:
                self._wait(eng, ev)


_UID = [0]


def _uname(name):
    _UID[0] += 1
    return "%s_u%d" % (name, _UID[0])


class Ring:
    def __init__(self, nc, stack, name, n, shape, dt):
        self.t = [stack.enter_context(nc.sbuf_tensor(_uname("%s%d" % (name, i)), shape, dt)) for i in range(n)]
        self.r = [Res("%s%d" % (name, i)) for i in range(n)]
        self.i = 0
        self.n = n

    def next(self):
        k = self.i % self.n
        self.i += 1
        return self.t[k], self.r[k]


def weight_list(layers):
    ws = []
    for l in layers:
        kind = l % 3
        ws.append(("wi_a%d" % l, D, 2 * FF))
        ws.append(("wo_a%d" % l, FF, D))
        if kind == 0:
            ws.append(("mix_in%d" % l, D, 5 * D))
        else:
            ws.append(("mix_in%d" % l, D, 3 * D))
        ws.append(("mix_out%d" % l, D, D))
        ws.append(("wi_b%d" % l, D, 2 * FF))
        ws.append(("wo_b%d" % l, FF, D))
    return ws


class Conv:
    PW = 1024

    def __init__(self, P):
        self.P = P
        self.tasks = []
        self.total = {}
        self.stored = {}
        for (n, K, N) in P.wl:
            cnt = 0
            for kc in range(K // 128):
                for n0 in range(0, N, self.PW):
                    self.tasks.append((n, kc, n0))
                    cnt += 1
            self.total[n] = cnt
            self.stored[n] = 0
        self.li = 0
        self.loaded = []
        self.casted = []
        self.stg = None
        self.stb = None

    def attach(self, stg, stb):
        self.stg, self.stb = stg, stb

    def detach(self):
        self.flush()
        self.stg = self.stb = None

    def step(self, allow_load=True):
        P, S, pw = self.P, self.P.S, self.PW
        can_load = allow_load and self.li < len(self.tasks)
        if self.casted and (len(self.casted) >= 2 or not can_load):
            (n, kc, n0), tb, rb = self.casted.pop(0)
            Wb = P.wb[n]
            dst = Wb[n0 // 256:(n0 + pw) // 256, :, kc * 256:(kc + 1) * 256].rearrange("nb p j -> p nb j")
            r = P.Rwb[n][self.stored[n] % 8]
            S.dma("sp", dst, tb[:, 0:pw].rearrange("p (nb j) -> p nb j", j=256), reads=[rb], writes=[r])
            self.stored[n] += 1
        if self.loaded and (len(self.loaded) >= 2 or not can_load):
            task, tf, rf = self.loaded.pop(0)
            tb, rb = self.stb.next()
            S.op("act", lambda e: e.copy(tb[:, 0:pw], tf[:, 0:pw]), reads=[rf], writes=[rb])
            self.casted.append((task, tb, rb))
        if can_load:
            task = self.tasks[self.li]
            self.li += 1
            n, kc, n0 = task
            tf, rf = self.stg.next()
            S.dma("sp", tf[:, 0:pw], P.din[n][kc * 128:(kc + 1) * 128, n0:n0 + pw], writes=[rf])
            self.loaded.append((task, tf, rf))

    def pump(self, k):
        if self.stg is None:
            return
        for _ in range(k):
            if self.li >= len(self.tasks) and not self.loaded and not self.casted:
                return
            self.step()

    def flush(self):
        while self.loaded or self.casted:
            self.step(allow_load=False)

    def ensure(self, n):
        while self.stored[n] < self.total[n]:
            assert self.stg is not None, "conversion rings not attached"
            self.step()


class Prog:
    def __init__(self, layers=(0, 1, 2, 3), phases=("ada", "ffn_a", "mix", "ffn_b"), dump=None):
        self.layers = list(layers)
        self.phases = set(phases)
        self.nc = bass.Bass("TRN2", target_bir_lowering=False)
        nc = self.nc
        self.din = {}
        self.wl = weight_list(self.layers)

        def din(name, shape, dt=F32):
            self.din[name] = nc.dram_tensor(name, list(shape), dt, kind="ExternalInput").ap()
            return self.din[name]

        self.xt_in = din("xt_in", [D, T])
        self.cols = din("cols", [128, self.ncols()])
        self.ident_in = din("ident", [128, 128])
        for (n, K, N) in self.wl:
            din(n, [K, N])
        if 2 in self.layers:
            din("bext", [32, 64, 960])
        if 0 in self.layers or 3 in self.layers:
            din("hmask", [32, 64])
        for l in self.layers:
            din("ada_down%d" % l, [D, 256])
            din("ada_up%d" % l, [256, 9 * D])
        self.out = nc.dram_tensor("out", [D, T], F32, kind="ExternalOutput").ap()
        self.XT = nc.dram_tensor("XT", [D, T], F32, kind="Internal").ap()
        self.wb = {}
        for (n, K, N) in self.wl:
            self.wb[n] = nc.dram_tensor("wb_" + n, [N // 256, 128, (K // 128) * 256], BF16, kind="Internal").ap()
        self.scrF = [nc.dram_tensor("scrF%d" % i, [D, T], F32, kind="Internal").ap() for i in range(6)]
        self.scrB = [nc.dram_tensor("scrB%d" % i, [D, T], BF16, kind="Internal").ap() for i in range(3)]
        self.scrTB = nc.dram_tensor("scrTB", [T, D], BF16, kind="Internal").ap()
        self.scrTF = [nc.dram_tensor("scrTF%d" % i, [T, D], F32, kind="Internal").ap() for i in range(2)]
        self.Rxt = [[Res("xt%d_%d" % (g, c)) for c in range(NCH)] for g in range(len(GROUPS))]
        self.Rwb = {n: [Res("wb_%s_%d" % (n, i)) for i in range(8)] for (n, K, N) in self.wl}
        self.conv = Conv(self)

    def ncols(self):
        return 64 + DEPTH * (9 * 32 + 3 * 32) + 3 * 32 + 2 * DEPTH * 32 + 8

    def col_off(self, what, l=0, j=0):
        if what == "c":
            return 0
        if what == "cctx":
            return 32
        base = 64 + l * (12 * 32)
        if what == "bias":
            return base + j * 32
        if what == "normw":
            return base + 9 * 32 + j * 32
        base2 = 64 + DEPTH * 12 * 32
        if what == "conv":
            return base2 + j * 32
        if what == "lb":
            return base2 + 96 + (l * 2 + j) * 32
        if what == "misc":
            return base2 + 96 + 2 * DEPTH * 32 + j
        raise KeyError(what)

    def build(self):
        nc = self.nc
        with contextlib.ExitStack() as top:
            S = self.S = Sch(nc, top)
            self.colsb = top.enter_context(nc.sbuf_tensor(_uname("colsb"), [128, self.ncols()], F32))
            self.Rcols = Res("cols")
            self.identf = top.enter_context(nc.sbuf_tensor(_uname("identf"), [128, 128], F32))
            self.identb = top.enter_context(nc.sbuf_tensor(_uname("identb"), [128, 128], BF16))
            self.onesb = top.enter_context(nc.sbuf_tensor(_uname("onesb"), [128, 128], BF16))
            self.Rconst = Res("const")
            self.mods = top.enter_context(nc.sbuf_tensor(_uname("mods"), [128, 2 * 9 * 32], F32))
            self.wmod = top.enter_context(nc.sbuf_tensor(_uname("wmod"), [128, 2 * 3 * 32], F32))
            self.Rmods = Res("mods")
            self.ps = [top.enter_context(nc.psum_tensor("ps%d" % i, [128, 512], F32)) for i in range(7)]
            self.Rps = [Res("ps%d" % i) for i in range(7)]
            self.pst = top.enter_context(nc.psum_tensor("pst", [128, 1024], BF16))
            self.Rpst = Res("pst")
            self.psi = 0

            S.dma("sp", self.colsb[:], self.cols[:, :], writes=[self.Rcols])
            S.dma("sp", self.identf[:], self.ident_in[:, :], writes=[self.Rconst])
            S.op("dve", lambda e: e.tensor_copy(self.identb[:], self.identf[:]), reads=[self.Rconst], writes=[self.Rconst])
            S.op("pool", lambda e: e.memset(self.onesb[:], 1.0), writes=[self.Rconst])
            Rall = [r for g in self.Rxt for r in g]
            for c0 in range(0, NCH, 4):
                S.dma("sp", self.XT[c0 * 128:(c0 + 4) * 128, :], self.xt_in[c0 * 128:(c0 + 4) * 128, :],
                      writes=[r for g in self.Rxt for r in g[c0:c0 + 4]])
            S.barrier()
            for l in self.layers:
                self.layer(l)
            S.barrier()
            for c0 in range(0, NCH, 4):
                S.dma("sp", self.out[c0 * 128:(c0 + 4) * 128, :], self.XT[c0 * 128:(c0 + 4) * 128, :],
                      reads=[r for g in self.Rxt for r in g[c0:c0 + 4]])
            S.barrier(engines=("sp",))
        return nc

    def next_ps(self, n=6):
        k = self.psi % n
        self.psi += 1
        return self.ps[k], self.Rps[k]

    def ada(self, l):
        nc, S = self.nc, self.S
        with contextlib.ExitStack() as st:
            sc = st.enter_context(nc.sbuf_tensor(_uname("ada_sc"), [128, 64], F32))
            Rsc = Res()
            tmp = st.enter_context(nc.sbuf_tensor(_uname("ada_tmp"), [128, 64], F32))
            dn = Ring(nc, st, "rg_adn", 2, [128, 8 * 256], F32)
            up = Ring(nc, st, "rg_aup", 2, [128, 2 * 2048], F32)
            t1 = st.enter_context(nc.sbuf_tensor(_uname("ada_t1"), [128, 4], F32))
            Rt1 = Res()
            cb = self.colsb
            for s, w in enumerate(("c", "cctx")):
                o = self.col_off(w)
                src = cb[:, o:o + 32]
                dst = sc[:, :].rearrange("p (c s) -> p c s", s=2)[:, :, s]
                t = tmp[:, s * 32:(s + 1) * 32]
                S.op("act", lambda e: e.activation(t, src, AF.Exp, scale=-1.0), reads=[self.Rcols], writes=[Rsc])
                S.op("dve", lambda e: e.tensor_scalar(t, t, 1.0, None, ALU.add), reads=[Rsc], writes=[Rsc])
                S.op("dve", lambda e: e.reciprocal(t, t), reads=[Rsc], writes=[Rsc])
                S.op("dve", lambda e: e.tensor_tensor(dst, t, src, ALU.mult), reads=[Rsc, self.Rcols], writes=[Rsc])
            down = self.din["ada_down%d" % l]
            pA, RA = self.ps[6], self.Rps[6]
            dts = []
            for q in range(4):
                tdn, rdn = dn.next()
                S.dma("sp", tdn[:, :].rearrange("p (c r) -> p c r", r=256),
                      down[q * 1024:(q + 1) * 1024, :].rearrange("(c p) r -> p c r", p=128), writes=[rdn])
                for rc in range(2):
                    for cc in range(8):
                        c = q * 8 + cc
                        S.op("pe", lambda e: e.matmul(pA[:, 8 * q + rc * 2:8 * q + rc * 2 + 2],
                                                      tdn[:, cc * 256 + rc * 128:cc * 256 + rc * 128 + 128], sc[:, 2 * c:2 * c + 2],
                                                      start=(cc == 0), stop=(cc == 7)),
                             reads=[rdn, Rsc], writes=[RA])
            S.op("dve", lambda e: e.tensor_copy(t1[:, :], pA[:, 0:4]), reads=[RA], writes=[Rt1])
            S.op("dve", lambda e: e.tensor_tensor(t1[:, :], t1[:, :], pA[:, 8:12], ALU.add), reads=[RA, Rt1], writes=[Rt1])
            S.op("dve", lambda e: e.tensor_tensor(t1[:, :], t1[:, :], pA[:, 16:20], ALU.add), reads=[RA, Rt1], writes=[Rt1])
            S.op("dve", lambda e: e.tensor_tensor(t1[:, :], t1[:, :], pA[:, 24:28], ALU.add), reads=[RA, Rt1], writes=[Rt1])
            upw = self.din["ada_up%d" % l]
            bo = self.col_off("bias", l, 0)
            for pn in range(18):
                tup, rup = up.next()
                S.dma("sp", tup[:, :].rearrange("p (r n) -> p r n", n=2048),
                      upw[:, pn * 2048:(pn + 1) * 2048].rearrange("(r p) n -> p r n", p=128), writes=[rup])
                pp, Rp = self.next_ps()
                for oc in range(16):
                    for rc in range(2):
                        S.op("pe", lambda e: e.matmul(pp[:, 2 * oc:2 * oc + 2],
                                                      tup[:, rc * 2048 + oc * 128:rc * 2048 + oc * 128 + 128],
                                                      t1[:, 2 * rc:2 * rc + 2], start=(rc == 0), stop=(rc == 1)),
                             reads=[rup, Rt1], writes=[Rp])
                for s in range(2):
                    dst = self.mods[:, s * 288 + pn * 16:s * 288 + pn * 16 + 16]
                    src = pp[:, 0:32].rearrange("p (o s) -> p o s", s=2)[:, :, s]
                    S.op("dve", lambda e: e.tensor_tensor(dst, src, cb[:, bo + pn * 16:bo + pn * 16 + 16], ALU.add),
                         reads=[Rp, self.Rcols], writes=[self.Rmods])
            for s in range(2):
                for j in range(3):
                    scale = self.mods[:, s * 288 + (3 * j + 1) * 32:s * 288 + (3 * j + 2) * 32]
                    nw = cb[:, self.col_off("normw", l, j):self.col_off("normw", l, j) + 32]
                    dst = self.wmod[:, (s * 3 + j) * 32:(s * 3 + j + 1) * 32]
                    S.op("dve", lambda e: e.scalar_tensor_tensor(dst, scale, 1.0, nw, ALU.add, ALU.mult),
                         reads=[self.Rmods, self.Rcols], writes=[self.Rmods])
                    if j != 1:
                        g = self.mods[:, s * 288 + (3 * j + 2) * 32:s * 288 + (3 * j + 3) * 32]
                        S.op("dve", lambda e: e.tensor_scalar(g, g, 0.5, None, ALU.mult), reads=[self.Rmods], writes=[self.Rmods])
            S.barrier()

    def mod_col(self, stream, j, c):
        o = stream * 288 + j * 32 + c
        return self.mods[:, o:o + 1]

    def wmod_col(self, stream, j, c):
        o = (stream * 3 + j) * 32 + c
        return self.wmod[:, o:o + 1]

    def norm_stage(self, B, g, j):
        nc, S = self.nc, self.S
        t0, Tt = GROUPS[g]
        stream = 1 if t0 >= TL else 0
        pss, Rpss = self.ps[6], self.Rps[6]
        for c in range(NCH):
            xc, rx = B["xc"].next()
            S.dma("sp", xc[:, 0:Tt], self.XT[c * 128:(c + 1) * 128, t0:t0 + Tt], reads=[self.Rxt[g][c]], writes=[rx])
            sq, rs = B["sq"].next()
            S.op("act", lambda e: e.activation(sq[:, 0:Tt], xc[:, 0:Tt], AF.Square), reads=[rx], writes=[rs])
            S.op("pe", lambda e: e.matmul(pss[:, 0:Tt], self.onesb[:], sq[:, 0:Tt], start=(c == 0), stop=(c == NCH - 1)),
                 reads=[rs, self.Rconst], writes=[Rpss])
        rstd = B["rstd"]
        S.op("act", lambda e: e.activation(rstd[:, 0:Tt], pss[:, 0:Tt], AF.Ln, bias=EPS, scale=1.0 / D),
             reads=[Rpss, self.Rconst], writes=[B["Rrstd"]])
        S.op("act", lambda e: e.activation(rstd[:, 0:Tt], rstd[:, 0:Tt], AF.Exp, scale=-0.5), reads=[B["Rrstd"]], writes=[B["Rrstd"]])
        hT = B["hT"]
        for c in range(NCH):
            xc, rx = B["xc"].next()
            S.dma("sp", xc[:, 0:Tt], self.XT[c * 128:(c + 1) * 128, t0:t0 + Tt], reads=[self.Rxt[g][c]], writes=[rx])
            tmp, rt = B["xo"].next()
            S.op("dve", lambda e: e.scalar_tensor_tensor(tmp[:, 0:Tt], xc[:, 0:Tt], self.wmod_col(stream, j, c), rstd[:, 0:Tt],
                                                         ALU.mult, ALU.mult),
                 reads=[rx, B["Rrstd"], self.Rmods], writes=[rt])
            S.op("dve", lambda e: e.tensor_scalar(hT[:, c * 512:c * 512 + Tt], tmp[:, 0:Tt], self.mod_col(stream, 3 * j, c), None, ALU.add),
                 reads=[rt, self.Rmods], writes=[B["RhT"]])

    def gemm(self, B, wname, blocks, KC, rhs, Rrhs, Tt, epi):
        S = self.S
        Wb = self.wb[wname]
        pending = None
        self.conv.ensure(wname)
        for blk in blocks:
            wt, rw = B["w"].next()
            S.dma("sp", wt[:, 0:KC * 256], Wb[blk, :, :], reads=self.Rwb[wname], writes=[rw])
            self.conv.pump(2)
            for half in range(2):
                ps, Rp = self.next_ps()
                for k in range(KC):
                    S.op("pe", lambda e: e.matmul(ps[:, 0:Tt], wt[:, k * 256 + half * 128:k * 256 + half * 128 + 128], rhs(k),
                                                  start=(k == 0), stop=(k == KC - 1)),
                         reads=[rw] + Rrhs, writes=[Rp])
                if pending is not None:
                    pending()
                pending = epi(blk * 2 + half, ps, Rp)
        if pending is not None:
            pending()

    def resid_epi(self, B, g, j, Tt, t0, stream):
        S = self.S

        def epi(c, ps, Rp):
            xc, rx = B["xc"].next()
            S.dma("sp", xc[:, 0:Tt], self.XT[c * 128:(c + 1) * 128, t0:t0 + Tt], reads=[self.Rxt[g][c]], writes=[rx])
            xo, ro = B["xo"].next()
            S.op("dve", lambda e: e.scalar_tensor_tensor(xo[:, 0:Tt], ps[:, 0:Tt], self.mod_col(stream, 3 * j + 2, c), xc[:, 0:Tt],
                                                         ALU.mult, ALU.add),
                 reads=[Rp, rx, self.Rmods], writes=[ro])
            S.dma("pool", self.XT[c * 128:(c + 1) * 128, t0:t0 + Tt], xo[:, 0:Tt], reads=[ro], writes=[self.Rxt[g][c]])
        return epi

    def alloc_main(self, st, conv=True):
        nc = self.nc
        B = {}
        if conv:
            self.conv.attach(Ring(nc, st, "cvf", 4, [128, Conv.PW], F32), Ring(nc, st, "cvb", 3, [128, Conv.PW], BF16))
        B["hT"] = st.enter_context(nc.sbuf_tensor(_uname("hT"), [128, NCH * 512], BF16))
        B["RhT"] = Res("hT")
        B["gT"] = st.enter_context(nc.sbuf_tensor(_uname("gT"), [128, 40 * 512], BF16))
        B["RgT"] = Res("gT")
        B["w"] = Ring(nc, st, "wbuf", 3, [128, 40 * 256], BF16)
        B["xc"] = Ring(nc, st, "xc", 4, [128, 512], F32)
        B["xo"] = Ring(nc, st, "xo", 4, [128, 512], F32)
        B["sq"] = Ring(nc, st, "sq", 3, [128, 512], BF16)
        B["sa"] = Ring(nc, st, "sa", 4, [128, 512], F32)
        B["rstd"] = st.enter_context(nc.sbuf_tensor(_uname("rstd"), [128, 512], F32))
        B["Rrstd"] = Res("rstd")
        return B

    def ffn(self, l, which, with_ctx):
        nc, S = self.nc, self.S
        j = 0 if which == "a" else 2
        wi, wo = "wi_%s%d" % (which, l), "wo_%s%d" % (which, l)
        with contextlib.ExitStack() as st:
            B = self.alloc_main(st)
            hT, gT = B["hT"], B["gT"]
            for g, (t0, Tt) in enumerate(GROUPS):
                stream = 1 if t0 >= TL else 0
                if stream == 1 and not with_ctx:
                    continue
                self.norm_stage(B, g, j)
                sa_of = {}

                def epi_wi(ch, ps, Rp):
                    if ch < 40:
                        sa, rsa = B["sa"].next()
                        S.op("act", lambda e: e.activation(sa[:, 0:Tt], ps[:, 0:Tt], AF.Silu), reads=[Rp], writes=[rsa])
                        sa_of[ch] = (sa, rsa)
                    else:
                        m = ch - 40
                        sa, rsa = sa_of.pop(m)
                        S.op("dve", lambda e: e.tensor_tensor(gT[:, m * 512:m * 512 + Tt], sa[:, 0:Tt], ps[:, 0:Tt], ALU.mult),
                             reads=[rsa, Rp], writes=[B["RgT"]])
                blocks = []
                for i in range(20):
                    blocks += [i, 20 + i]
                self.gemm(B, wi, blocks, NCH, lambda k: hT[:, k * 512:k * 512 + Tt], [B["RhT"]], Tt, epi_wi)
                self.gemm(B, wo, list(range(16)), 40, lambda k: gT[:, k * 512:k * 512 + Tt], [B["RgT"]], Tt,
                          self.resid_epi(B, g, j, Tt, t0, stream))
            self.conv.detach()
            S.barrier()

    def mix_conv(self, l):
        nc, S = self.nc, self.S
        Z, Bs = self.scrF[0], self.scrF[1]
        RZ, RB = Res("Z"), Res("Bs")
        wn_in, wn_out = "mix_in%d" % l, "mix_out%d" % l
        with contextlib.ExitStack() as st:
            B = self.alloc_main(st)
            hT = B["hT"]
            for g, (t0, Tt) in enumerate(GROUPS):
                self.norm_stage(B, g, 1)
                sa_of = {}

                def epi(ch, ps, Rp):
                    kc, h = ch // 32, ch % 32
                    if kc == 1:
                        sa, rsa = B["sa"].next()
                        S.op("act", lambda e: e.copy(sa[:, 0:Tt], ps[:, 0:Tt]), reads=[Rp], writes=[rsa])
                        sa_of[h] = (sa, rsa)
                    elif kc == 2:
                        sa, rsa = sa_of.pop(h)
                        xo, ro = B["xo"].next()
                        S.op("dve", lambda e: e.tensor_tensor(xo[:, 0:Tt], sa[:, 0:Tt], ps[:, 0:Tt], ALU.mult), reads=[rsa, Rp], writes=[ro])
                        S.dma("pool", Z[h * 128:(h + 1) * 128, t0:t0 + Tt], xo[:, 0:Tt], reads=[ro], writes=[])
                    else:
                        xo, ro = B["xo"].next()
                        S.op("act", lambda e: e.copy(xo[:, 0:Tt], ps[:, 0:Tt]), reads=[Rp], writes=[ro])
                        S.dma("pool", Bs[h * 128:(h + 1) * 128, t0:t0 + Tt], xo[:, 0:Tt], reads=[ro], writes=[])
                blocks = []
                for i in range(16):
                    blocks += [16 + i, 32 + i, i]
                self.gemm(B, wn_in, blocks, NCH, lambda k: hT[:, k * 512:k * 512 + Tt], [B["RhT"]], Tt, epi)
            S.barrier()
            ztr = Ring(nc, st, "ztr", 3, [128, 514], F32)
            for g, (t0, Tt) in enumerate(GROUPS):
                stream = 1 if t0 >= TL else 0
                left = g in (1, 2, 3)
                right = g in (0, 1, 2)
                for c in range(NCH):
                    zt, rz = ztr.next()
                    if not left:
                        S.op("pool", lambda e: e.memset(zt[:, 0:1], 0.0), writes=[rz])
                    if not right:
                        S.op("pool", lambda e: e.memset(zt[:, Tt + 1:Tt + 2], 0.0), writes=[rz])
                    lo = 0 if left else 1
                    hi = Tt + 2 if right else Tt + 1
                    S.dma("sp", zt[:, lo:hi], Z[c * 128:(c + 1) * 128, t0 - 1 + lo:t0 - 1 + hi], reads=[RZ], writes=[rz])
                    bc, rb = B["xc"].next()
                    S.dma("sp", bc[:, 0:Tt], Bs[c * 128:(c + 1) * 128, t0:t0 + Tt], reads=[RB], writes=[rb])
                    acc, ra = B["xo"].next()
                    cw = [self.colsb[:, self.col_off("conv", 0, j) + c:self.col_off("conv", 0, j) + c + 1] for j in range(3)]
                    S.op("dve", lambda e: e.tensor_scalar(acc[:, 0:Tt], zt[:, 0:Tt], cw[0], None, ALU.mult), reads=[rz, self.Rcols], writes=[ra])
                    S.op("dve", lambda e: e.scalar_tensor_tensor(acc[:, 0:Tt], zt[:, 1:Tt + 1], cw[1], acc[:, 0:Tt], ALU.mult, ALU.add),
                         reads=[rz, ra, self.Rcols], writes=[ra])
                    S.op("dve", lambda e: e.scalar_tensor_tensor(acc[:, 0:Tt], zt[:, 2:Tt + 2], cw[2], acc[:, 0:Tt], ALU.mult, ALU.add),
                         reads=[rz, ra, self.Rcols], writes=[ra])
                    S.op("dve", lambda e: e.tensor_tensor(hT[:, c * 512:c * 512 + Tt], acc[:, 0:Tt], bc[:, 0:Tt], ALU.mult),
                         reads=[ra, rb], writes=[B["RhT"]])
                self.gemm(B, wn_out, list(range(16)), NCH, lambda k: hT[:, k * 512:k * 512 + Tt], [B["RhT"]], Tt,
                          self.resid_epi(B, g, 1, Tt, t0, stream))
            self.conv.detach()
            S.barrier()

    def mix_na(self, l, update_ctx):
        nc, S = self.nc, self.S
        QT, KT, OT = self.scrB[0], self.scrB[1], self.scrB[2]
        VS = self.scrTB
        RQ, RK, RO, RV = Res("QT"), Res("KT"), Res("OT"), Res("VS")
        wn_in, wn_out = "mix_in%d" % l, "mix_out%d" % l
        mo = self.col_off("misc", 0, 0)
        with contextlib.ExitStack() as st:
            B = self.alloc_main(st)
            hT = B["hT"]
            bfr = Ring(nc, st, "bfr", 3, [128, 512], BF16)
            qws = st.enter_context(nc.sbuf_tensor(_uname("qws"), [128, 2], F32))
            Rqws = Res("qws")
            S.op("dve", lambda e: e.tensor_scalar(qws[:, 0:1], self.colsb[:, mo:mo + 1], float(128 ** -0.5), None, ALU.mult),
                 reads=[self.Rcols], writes=[Rqws])
            S.op("dve", lambda e: e.tensor_copy(qws[:, 1:2], self.colsb[:, mo + 1:mo + 2]), reads=[self.Rcols, Rqws], writes=[Rqws])
            pss, Rpss = self.ps[6], self.Rps[6]
            for g, (t0, Tt) in enumerate(GROUPS):
                self.norm_stage(B, g, 1)

                def epi(ch, ps, Rp):
                    kc, h = ch // 32, ch % 32
                    if kc < 2:
                        qf, rqf = B["sa"].next()
                        S.op("act", lambda e: e.copy(qf[:, 0:Tt], ps[:, 0:Tt]), reads=[Rp], writes=[rqf])
                        sqb, rsq = B["sq"].next()
                        S.op("act", lambda e: e.activation(sqb[:, 0:Tt], ps[:, 0:Tt], AF.Square), reads=[Rp], writes=[rsq])

                        def post():
                            S.op("pe", lambda e: e.matmul(pss[:, 0:Tt], self.onesb[:], sqb[:, 0:Tt], start=True, stop=True),
                                 reads=[rsq, self.Rconst], writes=[Rpss])
                            rr, rrr = B["xo"].next()
                            S.op("act", lambda e: e.activation(rr[:, 0:Tt], pss[:, 0:Tt], AF.Ln, bias=EPS, scale=1.0 / 128), reads=[Rpss], writes=[rrr])
                            S.op("act", lambda e: e.activation(rr[:, 0:Tt], rr[:, 0:Tt], AF.Exp, scale=-0.5), reads=[rrr], writes=[rrr])
                            qn, rqn = bfr.next()
                            S.op("dve", lambda e: e.scalar_tensor_tensor(qn[:, 0:Tt], qf[:, 0:Tt], qws[:, kc:kc + 1], rr[:, 0:Tt], ALU.mult, ALU.mult),
                                 reads=[rqf, rrr, Rqws], writes=[rqn])
                            dst, rd = (QT, RQ) if kc == 0 else (KT, RK)
                            S.dma("pool", dst[h * 128:(h + 1) * 128, t0:t0 + Tt], qn[:, 0:Tt], reads=[rqn], writes=[])
                        return post
                    else:
                        vb, rvb = B["sq"].next()
                        S.op("act", lambda e: e.copy(vb[:, 0:Tt], ps[:, 0:Tt]), reads=[Rp], writes=[rvb])

                        def post():
                            for tt in range(Tt // 128):
                                S.op("pe", lambda e: e.transpose(self.pst[:, tt * 128:(tt + 1) * 128], vb[:, tt * 128:(tt + 1) * 128], self.identb[:]),
                                     reads=[rvb, self.Rconst], writes=[self.Rpst])
                            vt, rvt = bfr.next()
                            S.op("dve", lambda e: e.tensor_copy(vt[:, 0:Tt], self.pst[:, 0:Tt]), reads=[self.Rpst], writes=[rvt])
                            S.dma("pool", VS[t0:t0 + Tt, h * 128:(h + 1) * 128].rearrange("(tt p) j -> p tt j", p=128),
                                  vt[:, 0:Tt].rearrange("p (tt j) -> p tt j", j=128), reads=[rvt], writes=[])
                        return post
                self.gemm(B, wn_in, list(range(48)), NCH, lambda k: hT[:, k * 512:k * 512 + Tt], [B["RhT"]], Tt, epi)
            self.conv.detach()
            S.barrier()
        with contextlib.ExitStack() as st:
            qtr = Ring(nc, st, "na_q", 2, [128, T], BF16)
            ktr = Ring(nc, st, "na_k", 2, [128, T], BF16)
            ver = Ring(nc, st, "na_ve", 2, [128, 18 * 128], BF16)
            vor = Ring(nc, st, "na_vo", 2, [128, 15 * 128], BF16)
            ber = Ring(nc, st, "na_be", 2, [64, 960], F32)
            scr = Ring(nc, st, "na_sc", 3, [128, 768], F32)
            pnr = Ring(nc, st, "na_pn", 3, [128, 768], BF16)
            ptr = Ring(nc, st, "na_pt", 3, [128, 384], BF16)
            str_ = Ring(nc, st, "na_st", 4, [128, 4], F32)
            otr = Ring(nc, st, "na_ot", 2, [128, T], BF16)
            Rpsth = [Res("pstA"), Res("pstB")]
            cnt = [0]
            bext = self.din["bext"]

            def softmax(psA, RA, psB, RB, n, wa, wb, badd):
                W = wa + wb
                sc, rsc = scr.next()
                stt, rst = str_.next()
                if badd is not None:
                    S.op("dve", lambda e: e.tensor_tensor(sc[0:n, 0:wa], psA[0:n, 0:wa], badd[0], ALU.add), reads=[RA, badd[1]], writes=[rsc])
                else:
                    S.op("act", lambda e: e.copy(sc[0:n, 0:wa], psA[0:n, 0:wa]), reads=[RA], writes=[rsc])
                if wb:
                    S.op("act", lambda e: e.copy(sc[0:n, wa:W], psB[0:n, 0:wb]), reads=[RB, rsc], writes=[rsc])
                S.op("pool", lambda e: e.memset(stt[:, :], 0.0), writes=[rst])
                S.op("dve", lambda e: e.reduce_max(stt[0:n, 0:1], sc[0:n, 0:W], AX.X), reads=[rsc, rst], writes=[rst])
                S.op("dve", lambda e: e.tensor_scalar(stt[0:n, 1:2], stt[0:n, 0:1], -1.0, None, ALU.mult), reads=[rst], writes=[rst])
                S.op("act", lambda e: e.activation(sc[0:n, 0:W], sc[0:n, 0:W], AF.Exp, bias=stt[0:n, 1:2], scale=1.0, accum_out=stt[0:n, 2:3]),
                     reads=[rsc, rst], writes=[rsc, rst])
                S.op("dve", lambda e: e.reciprocal(stt[0:n, 3:4], stt[0:n, 2:3]), reads=[rst], writes=[rst])
                pn, rpn = pnr.next()
                S.op("dve", lambda e: e.tensor_scalar(pn[0:n, 0:W], sc[0:n, 0:W], stt[0:n, 3:4], None, ALU.mult), reads=[rsc, rst], writes=[rpn])
                return pn, rpn

            def transposes(pn, rpn, n, nk):
                hf = cnt[0] % 2
                cnt[0] += 1
                off = hf * 512
                for kc in range(nk):
                    S.op("pe", lambda e: e.transpose(self.pst[:, off + kc * n:off + (kc + 1) * n], pn[0:n, kc * 128:(kc + 1) * 128],
                                                     self.identb[0:n, 0:n]),
                         reads=[rpn, self.Rconst], writes=[Rpsth[hf]])
                pT, rpt = ptr.next()
                S.op("act", lambda e: e.copy(pT[:, 0:nk * n], self.pst[:, off:off + nk * n]), reads=[Rpsth[hf]], writes=[rpt])
                return pT, rpt

            for h in range(32):
                qt, rq = qtr.next()
                kt, rk = ktr.next()
                ve, rve = ver.next()
                vo, rvo = vor.next()
                be, rbe = ber.next()
                oT, rot = otr.next()
                S.dma("sp", qt[:, :], QT[h * 128:(h + 1) * 128, :], reads=[RQ], writes=[rq])
                S.dma("sp", kt[:, :], KT[h * 128:(h + 1) * 128, :], reads=[RK], writes=[rk])
                S.dma("sp", ve[:, :].rearrange("p (j v) -> p j v", v=128),
                      VS[0:T, h * 128:(h + 1) * 128].rearrange("(j p) v -> p j v", p=128), reads=[RV], writes=[rve])
                S.dma("sp", vo[:, :].rearrange("p (j v) -> p j v", v=128),
                      VS[64:64 + 1920, h * 128:(h + 1) * 128].rearrange("(j p) v -> p j v", p=128), reads=[RV], writes=[rvo])
                S.dma("sp", be[:, :], bext[h, :, :], writes=[rbe])

                def stageA(r):
                    r0 = min(max(r - 4, 0), 24)
                    psA, RA = self.next_ps()
                    psB, RB = self.next_ps()
                    S.op("pe", lambda e: e.matmul(psA[0:64, 0:512], qt[:, 64 * r:64 * r + 64], kt[:, 64 * r0:64 * r0 + 512], start=True, stop=True),
                         reads=[rq, rk], writes=[RA])
                    S.op("pe", lambda e: e.matmul(psB[0:64, 0:256], qt[:, 64 * r:64 * r + 64], kt[:, TL:T], start=True, stop=True),
                         reads=[rq, rk], writes=[])
                    return (r, r0, psA, RA, psB, RB)

                def stageB(a):
                    r, r0, psA, RA, psB, RB = a
                    j0 = r0 - r + 7
                    pn, rpn = softmax(psA, RA, psB, RB, 64, 512, 256, (be[0:64, j0 * 64:j0 * 64 + 512], rbe))
                    return (r, r0, pn, rpn)

                def stageCD(b):
                    r, r0, pn, rpn = b
                    pT, rpt = transposes(pn, rpn, 64, 6)
                    psO, RO_ = self.next_ps()
                    for kc in range(6):
                        if kc < 4:
                            if r0 % 2 == 0:
                                vt, rv = ve[:, (r0 // 2 + kc) * 128:(r0 // 2 + kc + 1) * 128], rve
                            else:
                                vt, rv = vo[:, ((r0 - 1) // 2 + kc) * 128:((r0 - 1) // 2 + kc + 1) * 128], rvo
                        else:
                            vt, rv = ve[:, (16 + kc - 4) * 128:(16 + kc - 3) * 128], rve
                        S.op("pe", lambda e: e.matmul(psO[:, 0:64], vt, pT[:, kc * 64:(kc + 1) * 64], start=(kc == 0), stop=(kc == 5)),
                             reads=[rv, rpt], writes=[RO_])
                    S.op("dve", lambda e: e.tensor_copy(oT[:, 64 * r:64 * r + 64], psO[:, 0:64]), reads=[RO_], writes=[rot])

                prev = None
                for r in range(32):
                    a = stageA(r)
                    if prev is not None:
                        stageCD(prev)
                    prev = stageB(a)
                stageCD(prev)
                if update_ctx:
                    for i in range(2):
                        psA, RA = self.next_ps()
                        S.op("pe", lambda e: e.matmul(psA[:, 0:256], qt[:, TL + 128 * i:TL + 128 * i + 128], kt[:, TL:T], start=True, stop=True),
                             reads=[rq, rk], writes=[RA])
                        pn, rpn = softmax(psA, RA, None, None, 128, 256, 0, None)
                        pT, rpt = transposes(pn, rpn, 128, 2)
                        psO, RO_ = self.next_ps()
                        for kc in range(2):
                            S.op("pe", lambda e: e.matmul(psO[:, 0:128], ve[:, (16 + kc) * 128:(17 + kc) * 128], pT[:, kc * 128:(kc + 1) * 128],
                                                          start=(kc == 0), stop=(kc == 1)),
                                 reads=[rve, rpt], writes=[RO_])
                        S.op("dve", lambda e: e.tensor_copy(oT[:, TL + 128 * i:TL + 128 * i + 128], psO[:, 0:128]), reads=[RO_], writes=[rot])
                nt = T if update_ctx else TL
                S.dma("pool", OT[h * 128:(h + 1) * 128, 0:nt], oT[:, 0:nt], reads=[rot], writes=[])
            S.barrier()
        self.out_proj(l, OT, RO, update_ctx)

    def out_proj(self, l, MT, RM, with_ctx):
        nc, S = self.nc, self.S
        wn_out = "mix_out%d" % l
        with contextlib.ExitStack() as st:
            B = self.alloc_main(st)
            hT = B["hT"]
            for g, (t0, Tt) in enumerate(GROUPS):
                stream = 1 if t0 >= TL else 0
                if stream == 1 and not with_ctx:
                    continue
                S.dma("sp", hT[:, :].rearrange("p (c t) -> p c t", t=512)[:, :, 0:Tt],
                      MT[:, t0:t0 + Tt].rearrange("(c p) t -> p c t", p=128), reads=[RM], writes=[B["RhT"]])
                self.gemm(B, wn_out, list(range(16)), NCH, lambda k: hT[:, k * 512:k * 512 + Tt], [B["RhT"]], Tt,
                          self.resid_epi(B, g, 1, Tt, t0, stream))
            self.conv.detach()
            S.barrier()

    def mix_hgrn(self, l, update_ctx):
        nc, S = self.nc, self.S
        QS, LOGF, KK = self.scrF[0], [self.scrF[2], self.scrF[3]], [self.scrF[4], self.scrF[5]]
        SG = self.scrB[0]
        VS = self.scrTB
        OD = self.scrTF
        RQS, RSG, RLF, RKK, RV, ROD = Res("QS"), Res("SG"), Res("LF"), Res("KK"), Res("VS"), Res("OD")
        wn_in = "mix_in%d" % l
        mo = self.col_off("misc", 0, 0)
        gcol = self.colsb[:, mo + 2 + (0 if l == 0 else 1):mo + 3 + (0 if l == 0 else 1)]
        QSCALE = float(128 ** -0.5)
        with contextlib.ExitStack() as st:
            B = self.alloc_main(st)
            hT = B["hT"]
            bfr = Ring(nc, st, "bfr", 3, [128, 512], BF16)
            lbc = st.enter_context(nc.sbuf_tensor(_uname("lbc"), [128, 64], F32))
            Rlb = Res("lbc")
            if l == 0:
                S.op("pool", lambda e: e.memset(lbc[:, :], 0.0), writes=[Rlb])
            else:
                lt = st.enter_context(nc.sbuf_tensor(_uname("lbt"), [128, 6 * 64], F32))
                raw = [self.colsb[:, self.col_off("lb", i, 0):self.col_off("lb", i, 0) + 64] for i in range(DEPTH)]
                mx, ssum = lt[:, 0:64], lt[:, 64:128]
                S.op("dve", lambda e: e.tensor_tensor(mx, raw[0], raw[1], ALU.max), reads=[self.Rcols], writes=[Rlb])
                S.op("dve", lambda e: e.tensor_tensor(mx, mx, raw[2], ALU.max), reads=[self.Rcols, Rlb], writes=[Rlb])
                S.op("dve", lambda e: e.tensor_tensor(mx, mx, raw[3], ALU.max), reads=[self.Rcols, Rlb], writes=[Rlb])
                for i in range(DEPTH):
                    ei = lt[:, (2 + i) * 64:(3 + i) * 64]
                    S.op("dve", lambda e: e.tensor_tensor(ei, raw[i], mx, ALU.subtract), reads=[self.Rcols, Rlb], writes=[Rlb])
                    S.op("act", lambda e: e.activation(ei, ei, AF.Exp), reads=[Rlb], writes=[Rlb])
                e_ = [lt[:, (2 + i) * 64:(3 + i) * 64] for i in range(DEPTH)]
                S.op("dve", lambda e: e.tensor_tensor(ssum, e_[0], e_[1], ALU.add), reads=[Rlb], writes=[Rlb])
                S.op("dve", lambda e: e.tensor_tensor(ssum, ssum, e_[2], ALU.add), reads=[Rlb], writes=[Rlb])
                S.op("dve", lambda e: e.tensor_tensor(ssum, ssum, e_[3], ALU.add), reads=[Rlb], writes=[Rlb])
                S.op("dve", lambda e: e.reciprocal(ssum, ssum), reads=[Rlb], writes=[Rlb])
                acc = lt[:, 0:64]
                S.op("dve", lambda e: e.tensor_copy(acc, e_[1]), reads=[Rlb], writes=[Rlb])
                for i in range(2, l + 1):
                    S.op("dve", lambda e: e.tensor_tensor(acc, acc, e_[i], ALU.add), reads=[Rlb], writes=[Rlb])
                S.op("dve", lambda e: e.tensor_tensor(lbc[:, :], acc, ssum, ALU.mult), reads=[Rlb], writes=[Rlb])
            for g, (t0, Tt) in enumerate(GROUPS):
                self.norm_stage(B, g, 1)

                def epi(ch, ps, Rp):
                    kc, h = ch // 32, ch % 32
                    if kc == 0 or kc == 4:
                        if kc == 0:
                            xo, ro = B["xo"].next()
                            S.op("act", lambda e: e.activation(xo[:, 0:Tt], ps[:, 0:Tt], AF.Silu), reads=[Rp], writes=[ro])
                            S.dma("pool", QS[h * 128:(h + 1) * 128, t0:t0 + Tt], xo[:, 0:Tt], reads=[ro], writes=[])
                        else:
                            xb, rb = bfr.next()
                            S.op("act", lambda e: e.activation(xb[:, 0:Tt], ps[:, 0:Tt], AF.Silu), reads=[Rp], writes=[rb])
                            S.dma("pool", SG[h * 128:(h + 1) * 128, t0:t0 + Tt], xb[:, 0:Tt], reads=[rb], writes=[])
                        return None
                    if kc == 3:
                        vb, rvb = B["sq"].next()
                        S.op("act", lambda e: e.copy(vb[:, 0:Tt], ps[:, 0:Tt]), reads=[Rp], writes=[rvb])

                        def post():
                            for tt in range(Tt // 128):
                                S.op("pe", lambda e: e.transpose(self.pst[:, tt * 128:(tt + 1) * 128], vb[:, tt * 128:(tt + 1) * 128], self.identb[:]),
                                     reads=[rvb, self.Rconst], writes=[self.Rpst])
                            vt, rvt = bfr.next()
                            S.op("dve", lambda e: e.tensor_copy(vt[:, 0:Tt], self.pst[:, 0:Tt]), reads=[self.Rpst], writes=[rvt])
                            S.dma("pool", VS[t0:t0 + Tt, h * 128:(h + 1) * 128].rearrange("(tt p) j -> p tt j", p=128),
                                  vt[:, 0:Tt].rearrange("p (tt j) -> p tt j", j=128), reads=[rvt], writes=[])
                        return post
                    d = kc - 1
                    ee, re_ = B["sa"].next()
                    S.op("act", lambda e: e.activation(ee[:, 0:Tt], ps[:, 0:Tt], AF.Exp, scale=-1.0), reads=[Rp], writes=[re_])
                    l1, r1 = B["xo"].next()
                    S.op("act", lambda e: e.activation(l1[:, 0:Tt], ee[:, 0:Tt], AF.Ln, bias=1.0, scale=lbc[:, d * 32 + h:d * 32 + h + 1]),
                         reads=[re_, Rlb], writes=[r1])
                    S.op("act", lambda e: e.activation(ee[:, 0:Tt], ee[:, 0:Tt], AF.Ln, bias=1.0, scale=1.0), reads=[re_], writes=[re_])
                    S.op("dve", lambda e: e.tensor_tensor(l1[:, 0:Tt], l1[:, 0:Tt], ee[:, 0:Tt], ALU.subtract), reads=[r1, re_], writes=[r1])
                    S.dma("pool", LOGF[d][h * 128:(h + 1) * 128, t0:t0 + Tt], l1[:, 0:Tt], reads=[r1], writes=[])
                    S.op("act", lambda e: e.activation(ee[:, 0:Tt], l1[:, 0:Tt], AF.Exp), reads=[r1, re_], writes=[re_])
                    kk, rkk = B["xc"].next()
                    S.op("dve", lambda e: e.tensor_scalar(kk[:, 0:Tt], ee[:, 0:Tt], -1.0, 1.0, ALU.mult, ALU.add), reads=[re_], writes=[rkk])
                    S.dma("pool", KK[d][h * 128:(h + 1) * 128, t0:t0 + Tt], kk[:, 0:Tt], reads=[rkk], writes=[])
                    return None
                blocks = list(range(0, 16)) + list(range(64, 80)) + list(range(48, 64)) + list(range(16, 48))
                self.gemm(B, wn_in, blocks, NCH, lambda k: hT[:, k * 512:k * 512 + Tt], [B["RhT"]], Tt, epi)
            self.conv.ensure("mix_out%d" % l)
            self.conv.detach()
            S.barrier()
        HB = 8
        with contextlib.ExitStack() as st:
            ld = {n: Ring(nc, st, "hg_" + n, 2, [128, 512], F32) for n in ("qs", "lf", "kk", "G", "X", "aq", "ae", "Eq", "Ek", "Ee")}
            kendr = Ring(nc, st, "hg_kend", 2, [128, 512], BF16)
            qt = [st.enter_context(nc.sbuf_tensor(_uname("hg_qt%d" % i), [128, 512], BF16)) for i in range(HB)]
            kt = [st.enter_context(nc.sbuf_tensor(_uname("hg_kt%d" % i), [128, 512], BF16)) for i in range(HB)]
            keT = [st.enter_context(nc.sbuf_tensor(_uname("hg_keT%d" % i), [32, 16 * 128], BF16)) for i in range(HB)]
            vsb = [st.enter_context(nc.sbuf_tensor(_uname("hg_v%d" % i), [32, 16 * 128], BF16)) for i in range(HB)]
            Sst = [st.enter_context(nc.sbuf_tensor(_uname("hg_S%d" % i), [128, 128], F32)) for i in range(HB)]
            Sbf = [st.enter_context(nc.sbuf_tensor(_uname("hg_Sb%d" % i), [128, 128], BF16)) for i in range(HB)]
            dec = [st.enter_context(nc.sbuf_tensor(_uname("hg_dec%d" % i), [128, 16], F32)) for i in range(HB)]
            Rh = [{k: Res("%s%d" % (k, i)) for k in ("qt", "kt", "keT", "v", "S", "Sb", "dec")} for i in range(HB)]
            ones = st.enter_context(nc.sbuf_tensor(_uname("hg_ones"), [128, 512], F32))
            S.op("pool", lambda e: e.memset(ones[:, :], 1.0), writes=[self.Rconst])
            msk = st.enter_context(nc.sbuf_tensor(_uname("hg_msk"), [32, 64], F32))
            S.dma("sp", msk[:, :], self.din["hmask"][:, :], writes=[self.Rconst])
            scr_ = Ring(nc, st, "hg_sc", 8, [32, 32], BF16)
            obr = Ring(nc, st, "hg_ob", 16, [32, 128], F32)
            Rpsth = [Res("pstA"), Res("pstB")]
            for hb in range(32 // HB):
                for d in range(2):
                    for hh in range(HB):
                        S.op("pool", lambda e: e.memset(Sst[hh][:, :], 0.0), writes=[Rh[hh]["S"]])
                        S.op("pool", lambda e: e.memset(Sbf[hh][:, :], 0.0), writes=[Rh[hh]["Sb"]])
                    gorder = [4, 0, 1, 2, 3] if d == 0 else [4, 3, 2, 1, 0]
                    for g in gorder:
                        t0, Tt = GROUPS[g]
                        nch = Tt // 32
                        need_out = (g != 4) or update_ctx
                        for hh in range(HB):
                            h = hb * HB + hh
                            rows = slice(h * 128, (h + 1) * 128)
                            qs, rqs = ld["qs"].next()
                            lf, rlf = ld["lf"].next()
                            kk, rkk = ld["kk"].next()
                            S.dma("sp", qs[:, 0:Tt], QS[rows, t0:t0 + Tt], reads=[RQS], writes=[rqs])
                            S.dma("sp", lf[:, 0:Tt], LOGF[d][rows, t0:t0 + Tt], reads=[RLF], writes=[rlf])
                            S.dma("sp", kk[:, 0:Tt], KK[d][rows, t0:t0 + Tt], reads=[RKK], writes=[rkk])
                            S.dma("sp", vsb[hh][:, 0:nch * 128].rearrange("p (c v) -> p c v", v=128),
                                  VS[t0:t0 + Tt, rows].rearrange("(c p) v -> p c v", p=32), reads=[RV], writes=[Rh[hh]["v"]])
                            G, rG = ld["G"].next()
                            X, rX = ld["X"].next()
                            S.op("dve", lambda e: e.tensor_tensor_scan(G[:, 0:Tt], ones[:, 0:Tt], lf[:, 0:Tt], 0.0, ALU.mult, ALU.add),
                                 reads=[rlf, self.Rconst], writes=[rG])
                            S.op("dve", lambda e: e.tensor_tensor(X[:, 0:Tt], G[:, 0:Tt], lf[:, 0:Tt], ALU.subtract), reads=[rG, rlf], writes=[rX])
                            G3 = G[:, 0:Tt].rearrange("p (c t) -> p c t", t=32)
                            X3 = X[:, 0:Tt].rearrange("p (c t) -> p c t", t=32)
                            Gl = G3[:, :, 31:32]
                            Xf = X3[:, :, 0:1]
                            aq, raq = ld["aq"].next()
                            ae, rae = ld["ae"].next()
                            aq3 = aq[:, 0:Tt].rearrange("p (c t) -> p c t", t=32)
                            ae3 = ae[:, 0:Tt].rearrange("p (c t) -> p c t", t=32)
                            bshape = [128, nch, 32]
                            if d == 0:
                                S.op("dve", lambda e: e.tensor_tensor(aq3, G3, Xf.broadcast_to(bshape), ALU.subtract), reads=[rG, rX], writes=[raq])
                                S.op("dve", lambda e: e.tensor_tensor(ae3, Gl.broadcast_to(bshape), G3, ALU.subtract), reads=[rG], writes=[rae])
                            else:
                                S.op("dve", lambda e: e.tensor_tensor(aq3, Gl.broadcast_to(bshape), X3, ALU.subtract), reads=[rG, rX], writes=[raq])
                                S.op("dve", lambda e: e.tensor_tensor(ae3, X3, Xf.broadcast_to(bshape), ALU.subtract), reads=[rX], writes=[rae])
                            S.op("dve", lambda e: e.tensor_scalar(aq[:, 0:Tt], aq[:, 0:Tt], -80.0, None, ALU.max), reads=[raq], writes=[raq])
                            Eq, rEq = ld["Eq"].next()
                            Ek, rEk = ld["Ek"].next()
                            Ee, rEe = ld["Ee"].next()
                            S.op("act", lambda e: e.activation(Eq[:, 0:Tt], aq[:, 0:Tt], AF.Exp), reads=[raq], writes=[rEq])
                            S.op("act", lambda e: e.activation(Ek[:, 0:Tt], aq[:, 0:Tt], AF.Exp, scale=-1.0), reads=[raq], writes=[rEk])
                            S.op("act", lambda e: e.activation(Ee[:, 0:Tt], ae[:, 0:Tt], AF.Exp), reads=[rae], writes=[rEe])
                            S.op("dve", lambda e: e.tensor_tensor(dec[hh][:, 0:nch], G3[:, :, 31], X3[:, :, 0], ALU.subtract),
                                 reads=[rG, rX], writes=[Rh[hh]["dec"]])
                            S.op("act", lambda e: e.activation(dec[hh][:, 0:nch], dec[hh][:, 0:nch], AF.Exp), reads=[Rh[hh]["dec"]], writes=[Rh[hh]["dec"]])
                            S.op("dve", lambda e: e.scalar_tensor_tensor(qt[hh][:, 0:Tt], qs[:, 0:Tt], QSCALE, Eq[:, 0:Tt], ALU.mult, ALU.mult),
                                 reads=[rqs, rEq], writes=[Rh[hh]["qt"]])
                            S.op("dve", lambda e: e.tensor_tensor(kt[hh][:, 0:Tt], kk[:, 0:Tt], Ek[:, 0:Tt], ALU.mult), reads=[rkk, rEk], writes=[Rh[hh]["kt"]])
                            kend, rke = kendr.next()
                            S.op("dve", lambda e: e.tensor_tensor(kend[:, 0:Tt], kk[:, 0:Tt], Ee[:, 0:Tt], ALU.mult), reads=[rkk, rEe], writes=[rke])
                            for c8 in range(0, nch, 8):
                                hf = (c8 // 8) % 2
                                for c in range(c8, c8 + 8):
                                    S.op("pe", lambda e: e.transpose(self.pst[0:32, hf * 0 + (c - c8) * 128:(c - c8 + 1) * 128], kend[:, c * 32:(c + 1) * 32],
                                                                     self.identb[:, :]),
                                         reads=[rke, self.Rconst], writes=[self.Rpst])
                                S.op("act", lambda e: e.copy(keT[hh][0:32, c8 * 128:(c8 + 8) * 128], self.pst[0:32, 0:1024]),
                                     reads=[self.Rpst], writes=[Rh[hh]["keT"]])
                        corder = list(range(nch)) if d == 0 else list(range(nch - 1, -1, -1))
                        for c in corder:
                            c0 = c * 32
                            for hh in range(HB):
                                h = hb * HB + hh
                                R_ = Rh[hh]
                                if need_out:
                                    pS, RpS = self.next_ps()
                                    S.op("pe", lambda e: e.matmul(pS[0:32, 0:32], kt[hh][:, c0:c0 + 32], qt[hh][:, c0:c0 + 32], start=True, stop=True),
                                         reads=[R_["kt"], R_["qt"]], writes=[RpS])
                                    sc, rsc = scr_.next()
                                    S.op("dve", lambda e: e.tensor_tensor(sc[:, :], pS[0:32, 0:32], msk[:, d * 32:(d + 1) * 32], ALU.mult),
                                         reads=[RpS, self.Rconst], writes=[rsc])
                                    pO, RpO = self.next_ps()
                                    S.op("pe", lambda e: e.matmul(pO[0:32, 0:128], qt[hh][:, c0:c0 + 32], Sbf[hh][:, :], start=True, stop=False),
                                         reads=[R_["qt"], R_["Sb"]], writes=[RpO])
                                    S.op("pe", lambda e: e.matmul(pO[0:32, 0:128], sc[:, :], vsb[hh][0:32, c * 128:(c + 1) * 128], start=False, stop=True),
                                         reads=[rsc, R_["v"]], writes=[RpO])
                                    ob, rob = obr.next()
                                    S.op("act", lambda e: e.copy(ob[:, :], pO[0:32, 0:128]), reads=[RpO], writes=[rob])
                                    S.dma("sp", OD[d][t0 + c0:t0 + c0 + 32, h * 128:(h + 1) * 128], ob[:, :], reads=[rob], writes=[])
                                pT, RpT = self.next_ps()
                                S.op("pe", lambda e: e.matmul(pT[:, 0:128], keT[hh][0:32, c * 128:(c + 1) * 128], vsb[hh][0:32, c * 128:(c + 1) * 128],
                                                              start=True, stop=True),
                                     reads=[R_["keT"], R_["v"]], writes=[RpT])
                                S.op("dve", lambda e: e.scalar_tensor_tensor(Sst[hh][:, :], Sst[hh][:, :], dec[hh][:, c:c + 1], pT[:, 0:128], ALU.mult, ALU.add),
                                     reads=[R_["S"], R_["dec"], RpT], writes=[R_["S"]])
                                S.op("act", lambda e: e.copy(Sbf[hh][:, :], Sst[hh][:, :]), reads=[R_["S"]], writes=[R_["Sb"]])
            S.barrier()
        wn_out = "mix_out%d" % l
        with contextlib.ExitStack() as st:
            B = self.alloc_main(st, conv=False)
            hT, gT = B["hT"], B["gT"]
            ofr = Ring(nc, st, "hg_of", 2, [128, 2048], F32)
            obr2 = Ring(nc, st, "hg_ob2", 1, [128, 2048], F32)
            onr = Ring(nc, st, "hg_on", 1, [128, 2048], BF16)
            ssr = Ring(nc, st, "hg_ss", 2, [128, 16], F32)
            for g, (t0, Tt) in enumerate(GROUPS):
                stream = 1 if t0 >= TL else 0
                if stream == 1 and not update_ctx:
                    continue
                S.dma("sp", gT[:, 0:NCH * 512].rearrange("p (c t) -> p c t", t=512)[:, :, 0:Tt],
                      SG[:, t0:t0 + Tt].rearrange("(c p) t -> p c t", p=128), reads=[RSG], writes=[B["RgT"]])
                for tt in range(Tt // 128):
                    tk = t0 + tt * 128
                    for half in range(2):
                        of, rof = ofr.next()
                        ob, rob = obr2.next()
                        S.dma("sp", of[:, :], OD[0][tk:tk + 128, half * 2048:(half + 1) * 2048], reads=[ROD], writes=[rof])
                        S.dma("sp", ob[:, :], OD[1][tk:tk + 128, half * 2048:(half + 1) * 2048], reads=[ROD], writes=[rob])
                        S.op("dve", lambda e: e.tensor_tensor(of[:, :], of[:, :], ob[:, :], ALU.add), reads=[rof, rob], writes=[rof])
                        S.op("dve", lambda e: e.tensor_tensor(ob[:, :], of[:, :], of[:, :], ALU.mult), reads=[rof, rob], writes=[rob])
                        ss, rss = ssr.next()
                        S.op("dve", lambda e: e.tensor_reduce(ss[:, :], ob[:, :].rearrange("p (h v) -> p h v", v=128), AX.X, ALU.add), reads=[rob], writes=[rss])
                        S.op("act", lambda e: e.activation(ss[:, :], ss[:, :], AF.Ln, bias=EPS, scale=1.0 / 128), reads=[rss], writes=[rss])
                        S.op("act", lambda e: e.activation(ss[:, :], ss[:, :], AF.Exp, scale=-0.5), reads=[rss], writes=[rss])
                        on, ron = onr.next()
                        S.op("dve", lambda e: e.tensor_tensor(on[:, :].rearrange("p (h v) -> p h v", v=128), of[:, :].rearrange("p (h v) -> p h v", v=128),
                                                              ss[:, :].rearrange("p (h o) -> p h o", o=1).broadcast_to([128, 16, 128]), ALU.mult),
                             reads=[rof, rss], writes=[ron])
                        for h8 in range(2):
                            for k in range(8):
                                S.op("pe", lambda e: e.transpose(self.pst[:, k * 128:(k + 1) * 128], on[:, (h8 * 8 + k) * 128:(h8 * 8 + k + 1) * 128], self.identb[:, :]),
                                     reads=[ron, self.Rconst], writes=[self.Rpst])
                            for k in range(8):
                                hd = half * 16 + h8 * 8 + k
                                S.op("dve", lambda e: e.scalar_tensor_tensor(hT[:, hd * 512 + tt * 128:hd * 512 + tt * 128 + 128], self.pst[:, k * 128:(k + 1) * 128],
                                                                             gcol, gT[:, hd * 512 + tt * 128:hd * 512 + tt * 128 + 128], ALU.mult, ALU.mult),
                                     reads=[self.Rpst, self.Rcols, B["RgT"]], writes=[B["RhT"]])
                self.gemm(B, wn_out, list(range(16)), NCH, lambda k: hT[:, k * 512:k * 512 + Tt], [B["RhT"]], Tt,
                          self.resid_epi(B, g, 1, Tt, t0, stream))
            self.conv.detach()
            S.barrier()

    def layer(self, l):
        kind = l % 3
        update_ctx = l < DEPTH - 1
        ctx_needed = update_ctx or kind != 1
        if "ada" in self.phases:
            self.ada(l)
        if "ffn_a" in self.phases:
            self.ffn(l, "a", ctx_needed)
        if "mix" in self.phases:
            if kind == 1:
                self.mix_conv(l)
            elif kind == 2:
                self.mix_na(l, update_ctx)
            else:
                self.mix_hgrn(l, update_ctx)
        if "ffn_b" in self.phases:
            self.ffn(l, "b", update_ctx)


def host_cols(P, inputs, b):
    cols = np.zeros((128, P.ncols()), np.float32)

    def col(v):
        return np.ascontiguousarray(np.asarray(v, np.float32).reshape(32, 128).T)
    cols[:, 0:32] = col(inputs["c"][b])
    cols[:, 32:64] = col(inputs["c_ctx"])
    for l in range(DEPTH):
        for j in range(9):
            o = P.col_off("bias", l, j)
            cols[:, o:o + 32] = col(inputs["ada_bias"][l, j * D:(j + 1) * D])
        for j in range(3):
            o = P.col_off("normw", l, j)
            cols[:, o:o + 32] = col(inputs["norm_w"][l, j])
        for d in range(2):
            o = P.col_off("lb", l, d)
            cols[:, o:o + 32] = col(inputs["hgrn_lb"][l, d])
    for j in range(3):
        o = P.col_off("conv", 0, j)
        cols[:, o:o + 32] = col(inputs["sc_conv"][0, j])
    o = P.col_off("misc", 0, 0)
    cols[:, o] = inputs["na_q_norm"][0]
    cols[:, o + 1] = inputs["na_k_norm"][0]
    cols[:, o + 2] = inputs["hgrn_gnorm"][0]
    cols[:, o + 3] = inputs["hgrn_gnorm"][1]
    return cols


def host_weights(P, inputs):
    w = {}
    for l in P.layers:
        kind, jj = l % 3, l // 3
        w["wi_a%d" % l] = inputs["ffn_wi"][l, 0]
        w["wo_a%d" % l] = inputs["ffn_wo"][l, 0]
        w["wi_b%d" % l] = inputs["ffn_wi"][l, 1]
        w["wo_b%d" % l] = inputs["ffn_wo"][l, 1]
        if kind == 0:
            w["mix_in%d" % l] = inputs["hgrn_w_in"][jj]
            w["mix_out%d" % l] = inputs["hgrn_w_out"][jj]
        elif kind == 1:
            w["mix_in%d" % l] = inputs["sc_w_in"][jj]
            w["mix_out%d" % l] = inputs["sc_w_out"][jj]
        else:
            w["mix_in%d" % l] = inputs["na_w_qkv"][jj]
            w["mix_out%d" % l] = inputs["na_w_out"][jj]
        w["ada_down%d" % l] = inputs["ada_down"][l]
        w["ada_up%d" % l] = inputs["ada_up"][l]
    return w


def host_bext(rpb):
    qc = np.arange(64)[:, None]
    kc = np.arange(64)[None, :]
    ws = np.clip(qc - 8, 0, 48)
    valid = (kc >= ws) & (kc < ws + 16)
    bidx = np.clip(kc - qc + 15, 0, 30)
    g = rpb[:, :, bidx]
    g = np.transpose(g, (0, 2, 1, 3))
    out = np.where(valid[None, :, None, :], g, np.float32(-30000.0)).astype(np.float32)
    return np.ascontiguousarray(out.reshape(32, 64, 960))


def make_in_maps(P, inputs, cores):
    inputs = {k: np.asarray(v) for k, v in inputs.items()}
    w = host_weights(P, inputs)
    ident = np.eye(128, dtype=np.float32)
    bext = host_bext(inputs["na_rpb"][0]) if 2 in P.layers else None
    maps = []
    for b in cores:
        xt = np.ascontiguousarray(np.concatenate([inputs["x"][b], inputs["ctx"][b]], axis=0).T)
        m = {"xt_in": xt, "cols": host_cols(P, inputs, b), "ident": ident}
        if 2 in P.layers:
            m["bext"] = bext
        if 0 in P.layers or 3 in P.layers:
            si = np.arange(32)[:, None]
            ti = np.arange(32)[None, :]
            m["hmask"] = np.concatenate([(si <= ti), (si >= ti)], axis=1).astype(np.float32)
        m.update(w)
        maps.append(m)
    return maps


def kernel(**inputs):
    P = Prog()
    nc = P.build()
    maps = make_in_maps(P, inputs, list(range(N_CORES)))
    res = run_bass_kernel_spmd(nc, maps, core_ids=list(range(N_CORES)))
    out = np.stack([np.ascontiguousarray(np.asarray(r["out"])[:, 0:TL].T) for r in res.results], axis=0)
    return out.astype(np.float32)
```

```python
import contextlib
import numpy as np
import concourse.bass as bass
import concourse.mybir as mybir
from concourse.bass_utils import run_bass_kernel_spmd

F32 = mybir.dt.float32
BF16 = mybir.dt.bfloat16
AF = mybir.ActivationFunctionType
ALU = mybir.AluOpType
AX = mybir.AxisListType

D = 4096
NCH = 32
TL = 2048
TC = 256
T = TL + TC
FF = 5120
DEPTH = 4
EPS = 1e-6
GROUPS = [(0, 512), (512, 512), (1024, 512), (1536, 512), (2048, 256)]
N_CORES = 8


class Res:
    __slots__ = ("w", "r", "name")

    def __init__(self, name=""):
        self.w = None
        self.r = {}
        self.name = name


class Sch:
    NDQ = 8

    def __init__(self, nc, stack):
        self.nc = nc
        self.E = {"pe": nc.tensor, "act": nc.scalar, "dve": nc.vector, "pool": nc.gpsimd, "sp": nc.sync}
        self.csem = {}
        self.ccnt = {}
        self.semobj = {}
        for e in ("pe", "act", "dve", "pool"):
            self.csem[e] = stack.enter_context(nc.semaphore("c_" + e))
            self.ccnt[e] = 0
            self.semobj["c_" + e] = self.csem[e]
        self.dsem = {}
        self.dcnt = {}
        self.dlast = {}
        for q in ("sp", "pool", "act"):
            self.dsem[q] = [stack.enter_context(nc.semaphore("d_%s%d" % (q, i))) for i in range(self.NDQ)]
            self.dcnt[q] = 0
            self.dlast[q] = [None] * self.NDQ
            for i in range(self.NDQ):
                self.semobj["d_%s%d" % (q, i)] = self.dsem[q][i]
        self.seen = {e: {} for e in self.E}

    def _wait(self, eng, ev):
        key, val = ev
        s = self.seen[eng]
        if s.get(key, 0) >= val:
            return
        s[key] = val
        self.E[eng].wait_ge(self.semobj[key], val)

    def _deps(self, eng, reads, writes):
        skip = "c_pe" if eng == "pe" else None
        for r in reads:
            if r.w is not None and r.w[0] != skip:
                self._wait(eng, r.w)
        for w in writes:
            if w.w is not None and w.w[0] != skip:
                self._wait(eng, w.w)
            for k, v in w.r.items():
                if k != skip:
                    self._wait(eng, (k, v))

    def _commit(self, ev, reads, writes):
        k, v = ev
        for r in reads:
            if r.r.get(k, 0) < v:
                r.r[k] = v
        for w in writes:
            w.w = ev
            w.r = {}

    def op(self, eng, fn, reads=(), writes=()):
        self._deps(eng, reads, writes)
        ins = fn(self.E[eng])
        self.ccnt[eng] += 1
        ins.then_inc(self.csem[eng], 1)
        ev = ("c_" + eng, self.ccnt[eng])
        self._commit(ev, reads, writes)
        return ev

    def dma(self, q, out, in_, reads=(), writes=(), **kw):
        i = self.dcnt[q] % self.NDQ
        prev = self.dlast[q][i]
        if prev is not None:
            self._wait(q, prev)
        self._deps(q, reads, writes)
        ins = self.E[q].dma_start(out=out, in_=in_, **kw)
        self.dcnt[q] += 1
        key = "d_%s%d" % (q, i)
        val = 16 * ((self.dcnt[q] - 1) // self.NDQ + 1)
        ins.then_inc(self.dsem[q][i], 16)
        ev = (key, val)
        self.dlast[q][i] = ev
        self._commit(ev, reads, writes)
        return ev

    def barrier(self, engines=("pe", "act", "dve", "pool", "sp")):
        evs = []
        for e in ("pe", "act", "dve", "pool"):
            if self.ccnt[e]:
                evs.append(("c_" + e, self.ccnt[e]))
        for q in ("sp", "pool", "act"):
            for ev in self.dlast[q]:
                if ev is not None:
                    evs.append(ev)
        for eng in engines:
            for ev in evs:
                self._wait(eng, ev)


_UID = [0]


def _uname(name):
    _UID[0] += 1
    return "%s_u%d" % (name, _UID[0])


class Ring:
    def __init__(self, nc, stack, name, n, shape, dt):
        self.t = [stack.enter_context(nc.sbuf_tensor(_uname("%s%d" % (name, i)), shape, dt)) for i in range(n)]
        self.r = [Res("%s%d" % (name, i)) for i in range(n)]
        self.i = 0
        self.n = n

    def next(self):
        k = self.i % self.n
        self.i += 1
        return self.t[k], self.r[k]


def weight_list(layers):
    ws = []
    for l in layers:
        kind = l % 3
        ws.append(("wi_a%d" % l, D, 2 * FF))
        ws.append(("wo_a%d" % l, FF, D))
        if kind == 0:
            ws.append(("mix_in%d" % l, D, 5 * D))
        else:
            ws.append(("mix_in%d" % l, D, 3 * D))
        ws.append(("mix_out%d" % l, D, D))
        ws.append(("wi_b%d" % l, D, 2 * FF))
        ws.append(("wo_b%d" % l, FF, D))
    return ws


class Conv:
    PW = 1024

    def __init__(self, P):
        self.P = P
        self.tasks = []
        self.total = {}
        self.stored = {}
        for (n, K, N) in P.wl:
            cnt = 0
            for kc in range(K // 128):
                for n0 in range(0, N, self.PW):
                    self.tasks.append((n, kc, n0))
                    cnt += 1
            self.total[n] = cnt
            self.stored[n] = 0
        self.li = 0
        self.loaded = []
        self.casted = []
        self.stg = None
        self.stb = None

    def attach(self, stg, stb):
        self.stg, self.stb = stg, stb

    def detach(self):
        self.flush()
        self.stg = self.stb = None

    def step(self, allow_load=True):
        P, S, pw = self.P, self.P.S, self.PW
        can_load = allow_load and self.li < len(self.tasks)
        if self.casted and (len(self.casted) >= 2 or not can_load):
            (n, kc, n0), tb, rb = self.casted.pop(0)
            Wb = P.wb[n]
            dst = Wb[n0 // 256:(n0 + pw) // 256, :, kc * 256:(kc + 1) * 256].rearrange("nb p j -> p nb j")
            r = P.Rwb[n][self.stored[n] % 8]
            S.dma("sp", dst, tb[:, 0:pw].rearrange("p (nb j) -> p nb j", j=256), reads=[rb], writes=[r])
            self.stored[n] += 1
        if self.loaded and (len(self.loaded) >= 2 or not can_load):
            task, tf, rf = self.loaded.pop(0)
            tb, rb = self.stb.next()
            S.op("act", lambda e: e.copy(tb[:, 0:pw], tf[:, 0:pw]), reads=[rf], writes=[rb])
            self.casted.append((task, tb, rb))
        if can_load:
            task = self.tasks[self.li]
            self.li += 1
            n, kc, n0 = task
            tf, rf = self.stg.next()
            S.dma("sp", tf[:, 0:pw], P.din[n][kc * 128:(kc + 1) * 128, n0:n0 + pw], writes=[rf])
            self.loaded.append((task, tf, rf))

    def pump(self, k):
        if self.stg is None:
            return
        for _ in range(k):
            if self.li >= len(self.tasks) and not self.loaded and not self.casted:
                return
            self.step()

    def flush(self):
        while self.loaded or self.casted:
            self.step(allow_load=False)

    def ensure(self, n):
        while self.stored[n] < self.total[n]:
            assert self.stg is not None, "conversion rings not attached"
            self.step()


class Prog:
    def __init__(self, layers=(0, 1, 2, 3), phases=("ada", "ffn_a", "mix", "ffn_b"), dump=None):
        self.layers = list(layers)
        self.phases = set(phases)
        self.nc = bass.Bass("TRN2", target_bir_lowering=False)
        nc = self.nc
        self.din = {}
        self.wl = weight_list(self.layers)

        def din(name, shape, dt=F32):
            self.din[name] = nc.dram_tensor(name, list(shape), dt, kind="ExternalInput").ap()
            return self.din[name]

        self.xt_in = din("xt_in", [D, T])
        self.cols = din("cols", [128, self.ncols()])
        self.ident_in = din("ident", [128, 128])
        for (n, K, N) in self.wl:
            din(n, [K, N])
        if 2 in self.layers:
            din("bext", [32, 64, 960])
        if 0 in self.layers or 3 in self.layers:
            din("hmask", [32, 64])
        for l in self.layers:
            din("ada_down%d" % l, [D, 256])
            din("ada_up%d" % l, [256, 9 * D])
        self.out = nc.dram_tensor("out", [D, T], F32, kind="ExternalOutput").ap()
        self.XT = nc.dram_tensor("XT", [D, T], F32, kind="Internal").ap()
        self.wb = {}
        for (n, K, N) in self.wl:
            self.wb[n] = nc.dram_tensor("wb_" + n, [N // 256, 128, (K // 128) * 256], BF16, kind="Internal").ap()
        self.scrF = [nc.dram_tensor("scrF%d" % i, [D, T], F32, kind="Internal").ap() for i in range(6)]
        self.scrB = [nc.dram_tensor("scrB%d" % i, [D, T], BF16, kind="Internal").ap() for i in range(3)]
        self.scrTB = nc.dram_tensor("scrTB", [T, D], BF16, kind="Internal").ap()
        self.scrTF = [nc.dram_tensor("scrTF%d" % i, [T, D], F32, kind="Internal").ap() for i in range(2)]
        self.Rxt = [[Res("xt%d_%d" % (g, c)) for c in range(NCH)] for g in range(len(GROUPS))]
        self.Rwb = {n: [Res("wb_%s_%d" % (n, i)) for i in range(8)] for (n, K, N) in self.wl}
        self.conv = Conv(self)

    def ncols(self):
        return 64 + DEPTH * (9 * 32 + 3 * 32) + 3 * 32 + 2 * DEPTH * 32 + 8

    def col_off(self, what, l=0, j=0):
        if what == "c":
            return 0
        if what == "cctx":
            return 32
        base = 64 + l * (12 * 32)
        if what == "bias":
            return base + j * 32
        if what == "normw":
            return base + 9 * 32 + j * 32
        base2 = 64 + DEPTH * 12 * 32
        if what == "conv":
            return base2 + j * 32
        if what == "lb":
            return base2 + 96 + (l * 2 + j) * 32
        if what == "misc":
            return base2 + 96 + 2 * DEPTH * 32 + j
        raise KeyError(what)

    def build(self):
        nc = self.nc
        with contextlib.ExitStack() as top:
            S = self.S = Sch(nc, top)
            self.colsb = top.enter_context(nc.sbuf_tensor(_uname("colsb"), [128, self.ncols()], F32))
            self.Rcols = Res("cols")
            self.identf = top.enter_context(nc.sbuf_tensor(_uname("identf"), [128, 128], F32))
            self.identb = top.enter_context(nc.sbuf_tensor(_uname("identb"), [128, 128], BF16))
            self.onesb = top.enter_context(nc.sbuf_tensor(_uname("onesb"), [128, 128], BF16))
            self.Rconst = Res("const")
            self.mods = top.enter_context(nc.sbuf_tensor(_uname("mods"), [128, 2 * 9 * 32], F32))
            self.wmod = top.enter_context(nc.sbuf_tensor(_uname("wmod"), [128, 2 * 3 * 32], F32))
            self.Rmods = Res("mods")
            self.ps = [top.enter_context(nc.psum_tensor("ps%d" % i, [128, 512], F32)) for i in range(7)]
            self.Rps = [Res("ps%d" % i) for i in range(7)]
            self.pst = top.enter_context(nc.psum_tensor("pst", [128, 1024], BF16))
            self.Rpst = Res("pst")
            self.psi = 0

            S.dma("sp", self.colsb[:], self.cols[:, :], writes=[self.Rcols])
            S.dma("sp", self.identf[:], self.ident_in[:, :], writes=[self.Rconst])
            S.op("dve", lambda e: e.tensor_copy(self.identb[:], self.identf[:]), reads=[self.Rconst], writes=[self.Rconst])
            S.op("pool", lambda e: e.memset(self.onesb[:], 1.0), writes=[self.Rconst])
            Rall = [r for g in self.Rxt for r in g]
            for c0 in range(0, NCH, 4):
                S.dma("sp", self.XT[c0 * 128:(c0 + 4) * 128, :], self.xt_in[c0 * 128:(c0 + 4) * 128, :],
                      writes=[r for g in self.Rxt for r in g[c0:c0 + 4]])
            S.barrier()
            for l in self.layers:
                self.layer(l)
            S.barrier()
            for c0 in range(0, NCH, 4):
                S.dma("sp", self.out[c0 * 128:(c0 + 4) * 128, :], self.XT[c0 * 128:(c0 + 4) * 128, :],
                      reads=[r for g in self.Rxt for r in g[c0:c0 + 4]])
            S.barrier(engines=("sp",))
        return nc

    def next_ps(self, n=6):
        k = self.psi % n
        self.psi += 1
        return self.ps[k], self.Rps[k]

    def ada(self, l):
        nc, S = self.nc, self.S
        with contextlib.ExitStack() as st:
            sc = st.enter_context(nc.sbuf_tensor(_uname("ada_sc"), [128, 64], F32))
            Rsc = Res()
            tmp = st.enter_context(nc.sbuf_tensor(_uname("ada_tmp"), [128, 64], F32))
            dn = Ring(nc, st, "rg_adn", 2, [128, 8 * 256], F32)
            up = Ring(nc, st, "rg_aup", 2, [128, 2 * 2048], F32)
            t1 = st.enter_context(nc.sbuf_tensor(_uname("ada_t1"), [128, 4], F32))
            Rt1 = Res()
            cb = self.colsb
            for s, w in enumerate(("c", "cctx")):
                o = self.col_off(w)
                src = cb[:, o:o + 32]
                dst = sc[:, :].rearrange("p (c s) -> p c s", s=2)[:, :, s]
                t = tmp[:, s * 32:(s + 1) * 32]
                S.op("act", lambda e: e.activation(t, src, AF.Exp, scale=-1.0), reads=[self.Rcols], writes=[Rsc])
                S.op("dve", lambda e: e.tensor_scalar(t, t, 1.0, None, ALU.add), reads=[Rsc], writes=[Rsc])
                S.op("dve", lambda e: e.reciprocal(t, t), reads=[Rsc], writes=[Rsc])
                S.op("dve", lambda e: e.tensor_tensor(dst, t, src, ALU.mult), reads=[Rsc, self.Rcols], writes=[Rsc])
            down = self.din["ada_down%d" % l]
            pA, RA = self.ps[6], self.Rps[6]
            dts = []
            for q in range(4):
                tdn, rdn = dn.next()
                S.dma("sp", tdn[:, :].rearrange("p (c r) -> p c r", r=256),
                      down[q * 1024:(q + 1) * 1024, :].rearrange("(c p) r -> p c r", p=128), writes=[rdn])
                for rc in range(2):
                    for cc in range(8):
                        c = q * 8 + cc
                        S.op("pe", lambda e: e.matmul(pA[:, 8 * q + rc * 2:8 * q + rc * 2 + 2],
                                                      tdn[:, cc * 256 + rc * 128:cc * 256 + rc * 128 + 128], sc[:, 2 * c:2 * c + 2],
                                                      start=(cc == 0), stop=(cc == 7)),
                             reads=[rdn, Rsc], writes=[RA])
            S.op("dve", lambda e: e.tensor_copy(t1[:, :], pA[:, 0:4]), reads=[RA], writes=[Rt1])
            S.op("dve", lambda e: e.tensor_tensor(t1[:, :], t1[:, :], pA[:, 8:12], ALU.add), reads=[RA, Rt1], writes=[Rt1])
            S.op("dve", lambda e: e.tensor_tensor(t1[:, :], t1[:, :], pA[:, 16:20], ALU.add), reads=[RA, Rt1], writes=[Rt1])
            S.op("dve", lambda e: e.tensor_tensor(t1[:, :], t1[:, :], pA[:, 24:28], ALU.add), reads=[RA, Rt1], writes=[Rt1])
            upw = self.din["ada_up%d" % l]
            bo = self.col_off("bias", l, 0)
            for pn in range(18):
                tup, rup = up.next()
                S.dma("sp", tup[:, :].rearrange("p (r n) -> p r n", n=2048),
                      upw[:, pn * 2048:(pn + 1) * 2048].rearrange("(r p) n -> p r n", p=128), writes=[rup])
                pp, Rp = self.next_ps()
                for oc in range(16):
                    for rc in range(2):
                        S.op("pe", lambda e: e.matmul(pp[:, 2 * oc:2 * oc + 2],
                                                      tup[:, rc * 2048 + oc * 128:rc * 2048 + oc * 128 + 128],
                                                      t1[:, 2 * rc:2 * rc + 2], start=(rc == 0), stop=(rc == 1)),
                             reads=[rup, Rt1], writes=[Rp])
                for s in range(2):
                    dst = self.mods[:, s * 288 + pn * 16:s * 288 + pn * 16 + 16]
                    src = pp[:, 0:32].rearrange("p (o s) -> p o s", s=2)[:, :, s]
                    S.op("dve", lambda e: e.tensor_tensor(dst, src, cb[:, bo + pn * 16:bo + pn * 16 + 16], ALU.add),
                         reads=[Rp, self.Rcols], writes=[self.Rmods])
            for s in range(2):
                for j in range(3):
                    scale = self.mods[:, s * 288 + (3 * j + 1) * 32:s * 288 + (3 * j + 2) * 32]
                    nw = cb[:, self.col_off("normw", l, j):self.col_off("normw", l, j) + 32]
                    dst = self.wmod[:, (s * 3 + j) * 32:(s * 3 + j + 1) * 32]
                    S.op("dve", lambda e: e.scalar_tensor_tensor(dst, scale, 1.0, nw, ALU.add, ALU.mult),
                         reads=[self.Rmods, self.Rcols], writes=[self.Rmods])
                    if j != 1:
                        g = self.mods[:, s * 288 + (3 * j + 2) * 32:s * 288 + (3 * j + 3) * 32]
                        S.op("dve", lambda e: e.tensor_scalar(g, g, 0.5, None, ALU.mult), reads=[self.Rmods], writes=[self.Rmods])
            S.barrier()

    def mod_col(self, stream, j, c):
        o = stream * 288 + j * 32 + c
        return self.mods[:, o:o + 1]

    def wmod_col(self, stream, j, c):
        o = (stream * 3 + j) * 32 + c
        return self.wmod[:, o:o + 1]

    def norm_stage(self, B, g, j):
        nc, S = self.nc, self.S
        t0, Tt = GROUPS[g]
        stream = 1 if t0 >= TL else 0
        pss, Rpss = self.ps[6], self.Rps[6]
        for c in range(NCH):
            xc, rx = B["xc"].next()
            S.dma("sp", xc[:, 0:Tt], self.XT[c * 128:(c + 1) * 128, t0:t0 + Tt], reads=[self.Rxt[g][c]], writes=[rx])
            sq, rs = B["sq"].next()
            S.op("act", lambda e: e.activation(sq[:, 0:Tt], xc[:, 0:Tt], AF.Square), reads=[rx], writes=[rs])
            S.op("pe", lambda e: e.matmul(pss[:, 0:Tt], self.onesb[:], sq[:, 0:Tt], start=(c == 0), stop=(c == NCH - 1)),
                 reads=[rs, self.Rconst], writes=[Rpss])
        rstd = B["rstd"]
        S.op("act", lambda e: e.activation(rstd[:, 0:Tt], pss[:, 0:Tt], AF.Ln, bias=EPS, scale=1.0 / D),
             reads=[Rpss, self.Rconst], writes=[B["Rrstd"]])
        S.op("act", lambda e: e.activation(rstd[:, 0:Tt], rstd[:, 0:Tt], AF.Exp, scale=-0.5), reads=[B["Rrstd"]], writes=[B["Rrstd"]])
        hT = B["hT"]
        for c in range(NCH):
            xc, rx = B["xc"].next()
            S.dma("sp", xc[:, 0:Tt], self.XT[c * 128:(c + 1) * 128, t0:t0 + Tt], reads=[self.Rxt[g][c]], writes=[rx])
            tmp, rt = B["xo"].next()
            S.op("dve", lambda e: e.scalar_tensor_tensor(tmp[:, 0:Tt], xc[:, 0:Tt], self.wmod_col(stream, j, c), rstd[:, 0:Tt],
                                                         ALU.mult, ALU.mult),
                 reads=[rx, B["Rrstd"], self.Rmods], writes=[rt])
            S.op("dve", lambda e: e.tensor_scalar(hT[:, c * 512:c * 512 + Tt], tmp[:, 0:Tt], self.mod_col(stream, 3 * j, c), None, ALU.add),
                 reads=[rt, self.Rmods], writes=[B["RhT"]])

    def gemm(self, B, wname, blocks, KC, rhs, Rrhs, Tt, epi):
        S = self.S
        Wb = self.wb[wname]
        pending = None
        self.conv.ensure(wname)
        for blk in blocks:
            wt, rw = B["w"].next()
            S.dma("sp", wt[:, 0:KC * 256], Wb[blk, :, :], reads=self.Rwb[wname], writes=[rw])
            self.conv.pump(2)
            for half in range(2):
                ps, Rp = self.next_ps()
                for k in range(KC):
                    S.op("pe", lambda e: e.matmul(ps[:, 0:Tt], wt[:, k * 256 + half * 128:k * 256 + half * 128 + 128], rhs(k),
                                                  start=(k == 0), stop=(k == KC - 1)),
                         reads=[rw] + Rrhs, writes=[Rp])
                if pending is not None:
                    pending()
                pending = epi(blk * 2 + half, ps, Rp)
        if pending is not None:
            pending()

    def resid_epi(self, B, g, j, Tt, t0, stream):
        S = self.S

        def epi(c, ps, Rp):
            xc, rx = B["xc"].next()
            S.dma("sp", xc[:, 0:Tt], self.XT[c * 128:(c + 1) * 128, t0:t0 + Tt], reads=[self.Rxt[g][c]], writes=[rx])
            xo, ro = B["xo"].next()
            S.op("dve", lambda e: e.scalar_tensor_tensor(xo[:, 0:Tt], ps[:, 0:Tt], self.mod_col(stream, 3 * j + 2, c), xc[:, 0:Tt],
                                                         ALU.mult, ALU.add),
                 reads=[Rp, rx, self.Rmods], writes=[ro])
            S.dma("pool", self.XT[c * 128:(c + 1) * 128, t0:t0 + Tt], xo[:, 0:Tt], reads=[ro], writes=[self.Rxt[g][c]])
        return epi

    def alloc_main(self, st, conv=True):
        nc = self.nc
        B = {}
        if conv:
            self.conv.attach(Ring(nc, st, "cvf", 4, [128, Conv.PW], F32), Ring(nc, st, "cvb", 3, [128, Conv.PW], BF16))
        B["hT"] = st.enter_context(nc.sbuf_tensor(_uname("hT"), [128, NCH * 512], BF16))
        B["RhT"] = Res("hT")
        B["gT"] = st.enter_context(nc.sbuf_tensor(_uname("gT"), [128, 40 * 512], BF16))
        B["RgT"] = Res("gT")
        B["w"] = Ring(nc, st, "wbuf", 3, [128, 40 * 256], BF16)
        B["xc"] = Ring(nc, st, "xc", 4, [128, 512], F32)
        B["xo"] = Ring(nc, st, "xo", 4, [128, 512], F32)
        B["sq"] = Ring(nc, st, "sq", 3, [128, 512], BF16)
        B["sa"] = Ring(nc, st, "sa", 4, [128, 512], F32)
        B["rstd"] = st.enter_context(nc.sbuf_tensor(_uname("rstd"), [128, 512], F32))
        B["Rrstd"] = Res("rstd")
        return B

    def ffn(self, l, which, with_ctx):
        nc, S = self.nc, self.S
        j = 0 if which == "a" else 2
        wi, wo = "wi_%s%d" % (which, l), "wo_%s%d" % (which, l)
        with contextlib.ExitStack() as st:
            B = self.alloc_main(st)
            hT, gT = B["hT"], B["gT"]
            for g, (t0, Tt) in enumerate(GROUPS):
                stream = 1 if t0 >= TL else 0
                if stream == 1 and not with_ctx:
                    continue
                self.norm_stage(B, g, j)
                sa_of = {}

                def epi_wi(ch, ps, Rp):
                    if ch < 40:
                        sa, rsa = B["sa"].next()
                        S.op("act", lambda e: e.activation(sa[:, 0:Tt], ps[:, 0:Tt], AF.Silu), reads=[Rp], writes=[rsa])
                        sa_of[ch] = (sa, rsa)
                    else:
                        m = ch - 40
                        sa, rsa = sa_of.pop(m)
                        S.op("dve", lambda e: e.tensor_tensor(gT[:, m * 512:m * 512 + Tt], sa[:, 0:Tt], ps[:, 0:Tt], ALU.mult),
                             reads=[rsa, Rp], writes=[B["RgT"]])
                blocks = []
                for i in range(20):
                    blocks += [i, 20 + i]
                self.gemm(B, wi, blocks, NCH, lambda k: hT[:, k * 512:k * 512 + Tt], [B["RhT"]], Tt, epi_wi)
                self.gemm(B, wo, list(range(16)), 40, lambda k: gT[:, k * 512:k * 512 + Tt], [B["RgT"]], Tt,
                          self.resid_epi(B, g, j, Tt, t0, stream))
            self.conv.detach()
            S.barrier()

    def mix_conv(self, l):
        nc, S = self.nc, self.S
        Z, Bs = self.scrF[0], self.scrF[1]
        RZ, RB = Res("Z"), Res("Bs")
        wn_in, wn_out = "mix_in%d" % l, "mix_out%d" % l
        with contextlib.ExitStack() as st:
            B = self.alloc_main(st)
            hT = B["hT"]
            for g, (t0, Tt) in enumerate(GROUPS):
                self.norm_stage(B, g, 1)
                sa_of = {}

                def epi(ch, ps, Rp):
                    kc, h = ch // 32, ch % 32
                    if kc == 1:
                        sa, rsa = B["sa"].next()
                        S.op("act", lambda e: e.copy(sa[:, 0:Tt], ps[:, 0:Tt]), reads=[Rp], writes=[rsa])
                        sa_of[h] = (sa, rsa)
                    elif kc == 2:
                        sa, rsa = sa_of.pop(h)
                        xo, ro = B["xo"].next()
                        S.op("dve", lambda e: e.tensor_tensor(xo[:, 0:Tt], sa[:, 0:Tt], ps[:, 0:Tt], ALU.mult), reads=[rsa, Rp], writes=[ro])
                        S.dma("pool", Z[h * 128:(h + 1) * 128, t0:t0 + Tt], xo[:, 0:Tt], reads=[ro], writes=[])
                    else:
                        xo, ro = B["xo"].next()
                        S.op("act", lambda e: e.copy(xo[:, 0:Tt], ps[:, 0:Tt]), reads=[Rp], writes=[ro])
                        S.dma("pool", Bs[h * 128:(h + 1) * 128, t0:t0 + Tt], xo[:, 0:Tt], reads=[ro], writes=[])
                blocks = []
                for i in range(16):
                    blocks += [16 + i, 32 + i, i]
                self.gemm(B, wn_in, blocks, NCH, lambda k: hT[:, k * 512:k * 512 + Tt], [B["RhT"]], Tt, epi)
            S.barrier()
            ztr = Ring(nc, st, "ztr", 3, [128, 514], F32)
            for g, (t0, Tt) in enumerate(GROUPS):
                stream = 1 if t0 >= TL else 0
                left = g in (1, 2, 3)
                right = g in (0, 1, 2)
                for c in range(NCH):
                    zt, rz = ztr.next()
                    if not left:
                        S.op("pool", lambda e: e.memset(zt[:, 0:1], 0.0), writes=[rz])
                    if not right:
                        S.op("pool", lambda e: e.memset(zt[:, Tt + 1:Tt + 2], 0.0), writes=[rz])
                    lo = 0 if left else 1
                    hi = Tt + 2 if right else Tt + 1
                    S.dma("sp", zt[:, lo:hi], Z[c * 128:(c + 1) * 128, t0 - 1 + lo:t0 - 1 + hi], reads=[RZ], writes=[rz])
                    bc, rb = B["xc"].next()
                    S.dma("sp", bc[:, 0:Tt], Bs[c * 128:(c + 1) * 128, t0:t0 + Tt], reads=[RB], writes=[rb])
                    acc, ra = B["xo"].next()
                    cw = [self.colsb[:, self.col_off("conv", 0, j) + c:self.col_off("conv", 0, j) + c + 1] for j in range(3)]
                    S.op("dve", lambda e: e.tensor_scalar(acc[:, 0:Tt], zt[:, 0:Tt], cw[0], None, ALU.mult), reads=[rz, self.Rcols], writes=[ra])
                    S.op("dve", lambda e: e.scalar_tensor_tensor(acc[:, 0:Tt], zt[:, 1:Tt + 1], cw[1], acc[:, 0:Tt], ALU.mult, ALU.add),
                         reads=[rz, ra, self.Rcols], writes=[ra])
                    S.op("dve", lambda e: e.scalar_tensor_tensor(acc[:, 0:Tt], zt[:, 2:Tt + 2], cw[2], acc[:, 0:Tt], ALU.mult, ALU.add),
                         reads=[rz, ra, self.Rcols], writes=[ra])
                    S.op("dve", lambda e: e.tensor_tensor(hT[:, c * 512:c * 512 + Tt], acc[:, 0:Tt], bc[:, 0:Tt], ALU.mult),
                         reads=[ra, rb], writes=[B["RhT"]])
                self.gemm(B, wn_out, list(range(16)), NCH, lambda k: hT[:, k * 512:k * 512 + Tt], [B["RhT"]], Tt,
                          self.resid_epi(B, g, 1, Tt, t0, stream))
            self.conv.detach()
            S.barrier()

    def mix_na(self, l, update_ctx):
        nc, S = self.nc, self.S
        QT, KT, OT = self.scrB[0], self.scrB[1], self.scrB[2]
        VS = self.scrTB
        RQ, RK, RO, RV = Res("QT"), Res("KT"), Res("OT"), Res("VS")
        wn_in, wn_out = "mix_in%d" % l, "mix_out%d" % l
        mo = self.col_off("misc", 0, 0)
        with contextlib.ExitStack() as st:
            B = self.alloc_main(st)
            hT = B["hT"]
            bfr = Ring(nc, st, "bfr", 3, [128, 512], BF16)
            qws = st.enter_context(nc.sbuf_tensor(_uname("qws"), [128, 2], F32))
            Rqws = Res("qws")
            S.op("dve", lambda e: e.tensor_scalar(qws[:, 0:1], self.colsb[:, mo:mo + 1], float(128 ** -0.5), None, ALU.mult),
                 reads=[self.Rcols], writes=[Rqws])
            S.op("dve", lambda e: e.tensor_copy(qws[:, 1:2], self.colsb[:, mo + 1:mo + 2]), reads=[self.Rcols, Rqws], writes=[Rqws])
            pss, Rpss = self.ps[6], self.Rps[6]
            for g, (t0, Tt) in enumerate(GROUPS):
                self.norm_stage(B, g, 1)

                def epi(ch, ps, Rp):
                    kc, h = ch // 32, ch % 32
                    if kc < 2:
                        qf, rqf = B["sa"].next()
                        S.op("act", lambda e: e.copy(qf[:, 0:Tt], ps[:, 0:Tt]), reads=[Rp], writes=[rqf])
                        sqb, rsq = B["sq"].next()
                        S.op("act", lambda e: e.activation(sqb[:, 0:Tt], ps[:, 0:Tt], AF.Square), reads=[Rp], writes=[rsq])

                        def post():
                            S.op("pe", lambda e: e.matmul(pss[:, 0:Tt], self.onesb[:], sqb[:, 0:Tt], start=True, stop=True),
                                 reads=[rsq, self.Rconst], writes=[Rpss])
                            rr, rrr = B["xo"].next()
                            S.op("act", lambda e: e.activation(rr[:, 0:Tt], pss[:, 0:Tt], AF.Ln, bias=EPS, scale=1.0 / 128), reads=[Rpss], writes=[rrr])
                            S.op("act", lambda e: e.activation(rr[:, 0:Tt], rr[:, 0:Tt], AF.Exp, scale=-0.5), reads=[rrr], writes=[rrr])
                            qn, rqn = bfr.next()
                            S.op("dve", lambda e: e.scalar_tensor_tensor(qn[:, 0:Tt], qf[:, 0:Tt], qws[:, kc:kc + 1], rr[:, 0:Tt], ALU.mult, ALU.mult),
                                 reads=[rqf, rrr, Rqws], writes=[rqn])
                            dst, rd = (QT, RQ) if kc == 0 else (KT, RK)
                            S.dma("pool", dst[h * 128:(h + 1) * 128, t0:t0 + Tt], qn[:, 0:Tt], reads=[rqn], writes=[])
                        return post
                    else:
                        vb, rvb = B["sq"].next()
                        S.op("act", lambda e: e.copy(vb[:, 0:Tt], ps[:, 0:Tt]), reads=[Rp], writes=[rvb])

                        def post():
                            for tt in range(Tt // 128):
                                S.op("pe", lambda e: e.transpose(self.pst[:, tt * 128:(tt + 1) * 128], vb[:, tt * 128:(tt + 1) * 128], self.identb[:]),
                                     reads=[rvb, self.Rconst], writes=[self.Rpst])
                            vt, rvt = bfr.next()
                            S.op("dve", lambda e: e.tensor_copy(vt[:, 0:Tt], self.pst[:, 0:Tt]), reads=[self.Rpst], writes=[rvt])
                            S.dma("pool", VS[t0:t0 + Tt, h * 128:(h + 1) * 128].rearrange("(tt p) j -> p tt j", p=128),
                                  vt[:, 0:Tt].rearrange("p (tt j) -> p tt j", j=128), reads=[rvt], writes=[])
                        return post
                self.gemm(B, wn_in, list(range(48)), NCH, lambda k: hT[:, k * 512:k * 512 + Tt], [B["RhT"]], Tt, epi)
            self.conv.detach()
            S.barrier()
        with contextlib.ExitStack() as st:
            qtr = Ring(nc, st, "na_q", 2, [128, T], BF16)
            ktr = Ring(nc, st, "na_k", 2, [128, T], BF16)
            ver = Ring(nc, st, "na_ve", 2, [128, 18 * 128], BF16)
            vor = Ring(nc, st, "na_vo", 2, [128, 15 * 128], BF16)
            ber = Ring(nc, st, "na_be", 2, [64, 960], F32)
            scr = Ring(nc, st, "na_sc", 3, [128, 768], F32)
            pnr = Ring(nc, st, "na_pn", 3, [128, 768], BF16)
            ptr = Ring(nc, st, "na_pt", 3, [128, 384], BF16)
            str_ = Ring(nc, st, "na_st", 4, [128, 4], F32)
            otr = Ring(nc, st, "na_ot", 2, [128, T], BF16)
            Rpsth = [Res("pstA"), Res("pstB")]
            cnt = [0]
            bext = self.din["bext"]

            def softmax(psA, RA, psB, RB, n, wa, wb, badd):
                W = wa + wb
                sc, rsc = scr.next()
                stt, rst = str_.next()
                if badd is not None:
                    S.op("dve", lambda e: e.tensor_tensor(sc[0:n, 0:wa], psA[0:n, 0:wa], badd[0], ALU.add), reads=[RA, badd[1]], writes=[rsc])
                else:
                    S.op("act", lambda e: e.copy(sc[0:n, 0:wa], psA[0:n, 0:wa]), reads=[RA], writes=[rsc])
                if wb:
                    S.op("act", lambda e: e.copy(sc[0:n, wa:W], psB[0:n, 0:wb]), reads=[RB, rsc], writes=[rsc])
                S.op("pool", lambda e: e.memset(stt[:, :], 0.0), writes=[rst])
                S.op("dve", lambda e: e.reduce_max(stt[0:n, 0:1], sc[0:n, 0:W], AX.X), reads=[rsc, rst], writes=[rst])
                S.op("dve", lambda e: e.tensor_scalar(stt[0:n, 1:2], stt[0:n, 0:1], -1.0, None, ALU.mult), reads=[rst], writes=[rst])
                S.op("act", lambda e: e.activation(sc[0:n, 0:W], sc[0:n, 0:W], AF.Exp, bias=stt[0:n, 1:2], scale=1.0, accum_out=stt[0:n, 2:3]),
                     reads=[rsc, rst], writes=[rsc, rst])
                S.op("dve", lambda e: e.reciprocal(stt[0:n, 3:4], stt[0:n, 2:3]), reads=[rst], writes=[rst])
                pn, rpn = pnr.next()
                S.op("dve", lambda e: e.tensor_scalar(pn[0:n, 0:W], sc[0:n, 0:W], stt[0:n, 3:4], None, ALU.mult), reads=[rsc, rst], writes=[rpn])
                return pn, rpn

            def transposes(pn, rpn, n, nk):
                hf = cnt[0] % 2
                cnt[0] += 1
                off = hf * 512
                for kc in range(nk):
                    S.op("pe", lambda e: e.transpose(self.pst[:, off + kc * n:off + (kc + 1) * n], pn[0:n, kc * 128:(kc + 1) * 128],
                                                     self.identb[0:n, 0:n]),
                         reads=[rpn, self.Rconst], writes=[Rpsth[hf]])
                pT, rpt = ptr.next()
                S.op("act", lambda e: e.copy(pT[:, 0:nk * n], self.pst[:, off:off + nk * n]), reads=[Rpsth[hf]], writes=[rpt])
                return pT, rpt

            for h in range(32):
                qt, rq = qtr.next()
                kt, rk = ktr.next()
                ve, rve = ver.next()
                vo, rvo = vor.next()
                be, rbe = ber.next()
                oT, rot = otr.next()
                S.dma("sp", qt[:, :], QT[h * 128:(h + 1) * 128, :], reads=[RQ], writes=[rq])
                S.dma("sp", kt[:, :], KT[h * 128:(h + 1) * 128, :], reads=[RK], writes=[rk])
                S.dma("sp", ve[:, :].rearrange("p (j v) -> p j v", v=128),
                      VS[0:T, h * 128:(h + 1) * 128].rearrange("(j p) v -> p j v", p=128), reads=[RV], writes=[rve])
                S.dma("sp", vo[:, :].rearrange("p (j v) -> p j v", v=128),
                      VS[64:64 + 1920, h * 128:(h + 1) * 128].rearrange("(j p) v -> p j v", p=128), reads=[RV], writes=[rvo])
                S.dma("sp", be[:, :], bext[h, :, :], writes=[rbe])

                def stageA(r):
                    r0 = min(max(r - 4, 0), 24)
                    psA, RA = self.next_ps()
                    psB, RB = self.next_ps()
                    S.op("pe", lambda e: e.matmul(psA[0:64, 0:512], qt[:, 64 * r:64 * r + 64], kt[:, 64 * r0:64 * r0 + 512], start=True, stop=True),
                         reads=[rq, rk], writes=[RA])
                    S.op("pe", lambda e: e.matmul(psB[0:64, 0:256], qt[:, 64 * r:64 * r + 64], kt[:, TL:T], start=True, stop=True),
                         reads=[rq, rk], writes=[])
                    return (r, r0, psA, RA, psB, RB)

                def stageB(a):
                    r, r0, psA, RA, psB, RB = a
                    j0 = r0 - r + 7
                    pn, rpn = softmax(psA, RA, psB, RB, 64, 512, 256, (be[0:64, j0 * 64:j0 * 64 + 512], rbe))
                    return (r, r0, pn, rpn)

                def stageCD(b):
                    r, r0, pn, rpn = b
                    pT, rpt = transposes(pn, rpn, 64, 6)
                    psO, RO_ = self.next_ps()
                    for kc in range(6):
                        if kc < 4:
                            if r0 % 2 == 0:
                                vt, rv = ve[:, (r0 // 2 + kc) * 128:(r0 // 2 + kc + 1) * 128], rve
                            else:
                                vt, rv = vo[:, ((r0 - 1) // 2 + kc) * 128:((r0 - 1) // 2 + kc + 1) * 128], rvo
                        else:
                            vt, rv = ve[:, (16 + kc - 4) * 128:(16 + kc - 3) * 128], rve
                        S.op("pe", lambda e: e.matmul(psO[:, 0:64], vt, pT[:, kc * 64:(kc + 1) * 64], start=(kc == 0), stop=(kc == 5)),
                             reads=[rv, rpt], writes=[RO_])
                    S.op("dve", lambda e: e.tensor_copy(oT[:, 64 * r:64 * r + 64], psO[:, 0:64]), reads=[RO_], writes=[rot])

                prev = None
                for r in range(32):
                    a = stageA(r)
                    if prev is not None:
                        stageCD(prev)
                    prev = stageB(a)
                stageCD(prev)
                if update_ctx:
                    for i in range(2):
                        psA, RA = self.next_ps()
                        S.op("pe", lambda e: e.matmul(psA[:, 0:256], qt[:, TL + 128 * i:TL + 128 * i + 128], kt[:, TL:T], start=True, stop=True),
                             reads=[rq, rk], writes=[RA])
                        pn, rpn = softmax(psA, RA, None, None, 128, 256, 0, None)
                        pT, rpt = transposes(pn, rpn, 128, 2)
                        psO, RO_ = self.next_ps()
                        for kc in range(2):
                            S.op("pe", lambda e: e.matmul(psO[:, 0:128], ve[:, (16 + kc) * 128:(17 + kc) * 128], pT[:, kc * 128:(kc + 1) * 128],
                                                          start=(kc == 0), stop=(kc == 1)),
                                 reads=[rve, rpt], writes=[RO_])
                        S.op("dve", lambda e: e.tensor_copy(oT[:, TL + 128 * i:TL + 128 * i + 128], psO[:, 0:128]), reads=[RO_], writes=[rot])
                nt = T if update_ctx else TL
                S.dma("pool", OT[h * 128:(h + 1) * 128, 0:nt], oT[:, 0:nt], reads=[rot], writes=[])
            S.barrier()
        self.out_proj(l, OT, RO, update_ctx)

    def out_proj(self, l, MT, RM, with_ctx):
        nc, S = self.nc, self.S
        wn_out = "mix_out%d" % l
        with contextlib.ExitStack() as st:
            B = self.alloc_main(st)
            hT = B["hT"]
            for g, (t0, Tt) in enumerate(GROUPS):
                stream = 1 if t0 >= TL else 0
                if stream == 1 and not with_ctx:
                    continue
                S.dma("sp", hT[:, :].rearrange("p (c t) -> p c t", t=512)[:, :, 0:Tt],
                      MT[:, t0:t0 + Tt].rearrange("(c p) t -> p c t", p=128), reads=[RM], writes=[B["RhT"]])
                self.gemm(B, wn_out, list(range(16)), NCH, lambda k: hT[:, k * 512:k * 512 + Tt], [B["RhT"]], Tt,
                          self.resid_epi(B, g, 1, Tt, t0, stream))
            self.conv.detach()
            S.barrier()

    def mix_hgrn(self, l, update_ctx):
        nc, S = self.nc, self.S
        QS, LOGF, KK = self.scrF[0], [self.scrF[2], self.scrF[3]], [self.scrF[4], self.scrF[5]]
        SG = self.scrB[0]
        VS = self.scrTB
        OD = self.scrTF
        RQS, RSG, RLF, RKK, RV, ROD = Res("QS"), Res("SG"), Res("LF"), Res("KK"), Res("VS"), Res("OD")
        wn_in = "mix_in%d" % l
        mo = self.col_off("misc", 0, 0)
        gcol = self.colsb[:, mo + 2 + (0 if l == 0 else 1):mo + 3 + (0 if l == 0 else 1)]
        QSCALE = float(128 ** -0.5)
        with contextlib.ExitStack() as st:
            B = self.alloc_main(st)
            hT = B["hT"]
            bfr = Ring(nc, st, "bfr", 3, [128, 512], BF16)
            lbc = st.enter_context(nc.sbuf_tensor(_uname("lbc"), [128, 64], F32))
            Rlb = Res("lbc")
            if l == 0:
                S.op("pool", lambda e: e.memset(lbc[:, :], 0.0), writes=[Rlb])
            else:
                lt = st.enter_context(nc.sbuf_tensor(_uname("lbt"), [128, 6 * 64], F32))
                raw = [self.colsb[:, self.col_off("lb", i, 0):self.col_off("lb", i, 0) + 64] for i in range(DEPTH)]
                mx, ssum = lt[:, 0:64], lt[:, 64:128]
                S.op("dve", lambda e: e.tensor_tensor(mx, raw[0], raw[1], ALU.max), reads=[self.Rcols], writes=[Rlb])
                S.op("dve", lambda e: e.tensor_tensor(mx, mx, raw[2], ALU.max), reads=[self.Rcols, Rlb], writes=[Rlb])
                S.op("dve", lambda e: e.tensor_tensor(mx, mx, raw[3], ALU.max), reads=[self.Rcols, Rlb], writes=[Rlb])
                for i in range(DEPTH):
                    ei = lt[:, (2 + i) * 64:(3 + i) * 64]
                    S.op("dve", lambda e: e.tensor_tensor(ei, raw[i], mx, ALU.subtract), reads=[self.Rcols, Rlb], writes=[Rlb])
                    S.op("act", lambda e: e.activation(ei, ei, AF.Exp), reads=[Rlb], writes=[Rlb])
                e_ = [lt[:, (2 + i) * 64:(3 + i) * 64] for i in range(DEPTH)]
                S.op("dve", lambda e: e.tensor_tensor(ssum, e_[0], e_[1], ALU.add), reads=[Rlb], writes=[Rlb])
                S.op("dve", lambda e: e.tensor_tensor(ssum, ssum, e_[2], ALU.add), reads=[Rlb], writes=[Rlb])
                S.op("dve", lambda e: e.tensor_tensor(ssum, ssum, e_[3], ALU.add), reads=[Rlb], writes=[Rlb])
                S.op("dve", lambda e: e.reciprocal(ssum, ssum), reads=[Rlb], writes=[Rlb])
                acc = lt[:, 0:64]
                S.op("dve", lambda e: e.tensor_copy(acc, e_[1]), reads=[Rlb], writes=[Rlb])
                for i in range(2, l + 1):
                    S.op("dve", lambda e: e.tensor_tensor(acc, acc, e_[i], ALU.add), reads=[Rlb], writes=[Rlb])
                S.op("dve", lambda e: e.tensor_tensor(lbc[:, :], acc, ssum, ALU.mult), reads=[Rlb], writes=[Rlb])
            for g, (t0, Tt) in enumerate(GROUPS):
                self.norm_stage(B, g, 1)

                def epi(ch, ps, Rp):
                    kc, h = ch // 32, ch % 32
                    if kc == 0 or kc == 4:
                        if kc == 0:
                            xo, ro = B["xo"].next()
                            S.op("act", lambda e: e.activation(xo[:, 0:Tt], ps[:, 0:Tt], AF.Silu), reads=[Rp], writes=[ro])
                            S.dma("pool", QS[h * 128:(h + 1) * 128, t0:t0 + Tt], xo[:, 0:Tt], reads=[ro], writes=[])
                        else:
                            xb, rb = bfr.next()
                            S.op("act", lambda e: e.activation(xb[:, 0:Tt], ps[:, 0:Tt], AF.Silu), reads=[Rp], writes=[rb])
                            S.dma("pool", SG[h * 128:(h + 1) * 128, t0:t0 + Tt], xb[:, 0:Tt], reads=[rb], writes=[])
                        return None
                    if kc == 3:
                        vb, rvb = B["sq"].next()
                        S.op("act", lambda e: e.copy(vb[:, 0:Tt], ps[:, 0:Tt]), reads=[Rp], writes=[rvb])

                        def post():
                            for tt in range(Tt // 128):
                                S.op("pe", lambda e: e.transpose(self.pst[:, tt * 128:(tt + 1) * 128], vb[:, tt * 128:(tt + 1) * 128], self.identb[:]),
                                     reads=[rvb, self.Rconst], writes=[self.Rpst])
                            vt, rvt = bfr.next()
                            S.op("dve", lambda e: e.tensor_copy(vt[:, 0:Tt], self.pst[:, 0:Tt]), reads=[self.Rpst], writes=[rvt])
                            S.dma("pool", VS[t0:t0 + Tt, h * 128:(h + 1) * 128].rearrange("(tt p) j -> p tt j", p=128),
                                  vt[:, 0:Tt].rearrange("p (tt j) -> p tt j", j=128), reads=[rvt], writes=[])
                        return post
                    d = kc - 1
                    ee, re_ = B["sa"].next()
                    S.op("act", lambda e: e.activation(ee[:, 0:Tt], ps[:, 0:Tt], AF.Exp, scale=-1.0), reads=[Rp], writes=[re_])
                    l1, r1 = B["xo"].next()
                    S.op("act", lambda e: e.activation(l1[:, 0:Tt], ee[:, 0:Tt], AF.Ln, bias=1.0, scale=lbc[:, d * 32 + h:d * 32 + h + 1]),
                         reads=[re_, Rlb], writes=[r1])
                    S.op("act", lambda e: e.activation(ee[:, 0:Tt], ee[:, 0:Tt], AF.Ln, bias=1.0, scale=1.0), reads=[re_], writes=[re_])
                    S.op("dve", lambda e: e.tensor_tensor(l1[:, 0:Tt], l1[:, 0:Tt], ee[:, 0:Tt], ALU.subtract), reads=[r1, re_], writes=[r1])
                    S.dma("pool", LOGF[d][h * 128:(h + 1) * 128, t0:t0 + Tt], l1[:, 0:Tt], reads=[r1], writes=[])
                    S.op("act", lambda e: e.activation(ee[:, 0:Tt], l1[:, 0:Tt], AF.Exp), reads=[r1, re_], writes=[re_])
                    kk, rkk = B["xc"].next()
                    S.op("dve", lambda e: e.tensor_scalar(kk[:, 0:Tt], ee[:, 0:Tt], -1.0, 1.0, ALU.mult, ALU.add), reads=[re_], writes=[rkk])
                    S.dma("pool", KK[d][h * 128:(h + 1) * 128, t0:t0 + Tt], kk[:, 0:Tt], reads=[rkk], writes=[])
                    return None
                blocks = list(range(0, 16)) + list(range(64, 80)) + list(range(48, 64)) + list(range(16, 48))
                self.gemm(B, wn_in, blocks, NCH, lambda k: hT[:, k * 512:k * 512 + Tt], [B["RhT"]], Tt, epi)
            self.conv.ensure("mix_out%d" % l)
            self.conv.detach()
            S.barrier()
        HB = 8
        with contextlib.ExitStack() as st:
            ld = {n: Ring(nc, st, "hg_" + n, 2, [128, 512], F32) for n in ("qs", "lf", "kk", "G", "X", "aq", "ae", "Eq", "Ek", "Ee")}
            kendr = Ring(nc, st, "hg_kend", 2, [128, 512], BF16)
            qt = [st.enter_context(nc.sbuf_tensor(_uname("hg_qt%d" % i), [128, 512], BF16)) for i in range(HB)]
            kt = [st.enter_context(nc.sbuf_tensor(_uname("hg_kt%d" % i), [128, 512], BF16)) for i in range(HB)]
            keT = [st.enter_context(nc.sbuf_tensor(_uname("hg_keT%d" % i), [32, 16 * 128], BF16)) for i in range(HB)]
            vsb = [st.enter_context(nc.sbuf_tensor(_uname("hg_v%d" % i), [32, 16 * 128], BF16)) for i in range(HB)]
            Sst = st.enter_context(nc.sbuf_tensor(_uname("hg_S"), [128, HB * 128], F32))
            Sbf = st.enter_context(nc.sbuf_tensor(_uname("hg_Sb"), [128, HB * 128], BF16))
            RSb = Res("Sb")
            ci = [0]
            dec = [st.enter_context(nc.sbuf_tensor(_uname("hg_dec%d" % i), [128, 16], F32)) for i in range(HB)]
            Rh = [{k: Res("%s%d" % (k, i)) for k in ("qt", "kt", "keT", "v", "S", "Sb", "dec")} for i in range(HB)]
            ones = st.enter_context(nc.sbuf_tensor(_uname("hg_ones"), [128, 512], F32))
            S.op("pool", lambda e: e.memset(ones[:, :], 1.0), writes=[self.Rconst])
            msk = st.enter_context(nc.sbuf_tensor(_uname("hg_msk"), [32, 64], F32))
            S.dma("sp", msk[:, :], self.din["hmask"][:, :], writes=[self.Rconst])
            scr_ = Ring(nc, st, "hg_sc", 4, [32, HB * 32], BF16)
            obr = Ring(nc, st, "hg_ob", 6, [32, 512], F32)
            Rpsth = [Res("pstA"), Res("pstB")]
            for hb in range(32 // HB):
                for d in range(2):
                    S.op("pool", lambda e: e.memset(Sst[:, :], 0.0), writes=[Rh[hh]["S"] for hh in range(HB)])
                    S.op("pool", lambda e: e.memset(Sbf[:, :], 0.0), writes=[RSb])
                    gorder = [4, 0, 1, 2, 3] if d == 0 else [4, 3, 2, 1, 0]
                    for g in gorder:
                        t0, Tt = GROUPS[g]
                        nch = Tt // 32
                        need_out = (g != 4) or update_ctx
                        for hh in range(HB):
                            h = hb * HB + hh
                            rows = slice(h * 128, (h + 1) * 128)
                            qs, rqs = ld["qs"].next()
                            lf, rlf = ld["lf"].next()
                            kk, rkk = ld["kk"].next()
                            S.dma("sp", qs[:, 0:Tt], QS[rows, t0:t0 + Tt], reads=[RQS], writes=[rqs])
                            S.dma("sp", lf[:, 0:Tt], LOGF[d][rows, t0:t0 + Tt], reads=[RLF], writes=[rlf])
                            S.dma("sp", kk[:, 0:Tt], KK[d][rows, t0:t0 + Tt], reads=[RKK], writes=[rkk])
                            S.dma("sp", vsb[hh][:, 0:nch * 128].rearrange("p (c v) -> p c v", v=128),
                                  VS[t0:t0 + Tt, rows].rearrange("(c p) v -> p c v", p=32), reads=[RV], writes=[Rh[hh]["v"]])
                            G, rG = ld["G"].next()
                            X, rX = ld["X"].next()
                            S.op("dve", lambda e: e.tensor_tensor_scan(G[:, 0:Tt], ones[:, 0:Tt], lf[:, 0:Tt], 0.0, ALU.mult, ALU.add),
                                 reads=[rlf, self.Rconst], writes=[rG])
                            S.op("dve", lambda e: e.tensor_tensor(X[:, 0:Tt], G[:, 0:Tt], lf[:, 0:Tt], ALU.subtract), reads=[rG, rlf], writes=[rX])
                            G3 = G[:, 0:Tt].rearrange("p (c t) -> p c t", t=32)
                            X3 = X[:, 0:Tt].rearrange("p (c t) -> p c t", t=32)
                            Gl = G3[:, :, 31:32]
                            Xf = X3[:, :, 0:1]
                            aq, raq = ld["aq"].next()
                            ae, rae = ld["ae"].next()
                            aq3 = aq[:, 0:Tt].rearrange("p (c t) -> p c t", t=32)
                            ae3 = ae[:, 0:Tt].rearrange("p (c t) -> p c t", t=32)
                            bshape = [128, nch, 32]
                            if d == 0:
                                S.op("dve", lambda e: e.tensor_tensor(aq3, G3, Xf.broadcast_to(bshape), ALU.subtract), reads=[rG, rX], writes=[raq])
                                S.op("dve", lambda e: e.tensor_tensor(ae3, Gl.broadcast_to(bshape), G3, ALU.subtract), reads=[rG], writes=[rae])
                            else:
                                S.op("dve", lambda e: e.tensor_tensor(aq3, Gl.broadcast_to(bshape), X3, ALU.subtract), reads=[rG, rX], writes=[raq])
                                S.op("dve", lambda e: e.tensor_tensor(ae3, X3, Xf.broadcast_to(bshape), ALU.subtract), reads=[rX], writes=[rae])
                            S.op("dve", lambda e: e.tensor_scalar(aq[:, 0:Tt], aq[:, 0:Tt], -80.0, None, ALU.max), reads=[raq], writes=[raq])
                            Eq, rEq = ld["Eq"].next()
                            Ek, rEk = ld["Ek"].next()
                            Ee, rEe = ld["Ee"].next()
                            S.op("act", lambda e: e.activation(Eq[:, 0:Tt], aq[:, 0:Tt], AF.Exp), reads=[raq], writes=[rEq])
                            S.op("act", lambda e: e.activation(Ek[:, 0:Tt], aq[:, 0:Tt], AF.Exp, scale=-1.0), reads=[raq], writes=[rEk])
                            S.op("act", lambda e: e.activation(Ee[:, 0:Tt], ae[:, 0:Tt], AF.Exp), reads=[rae], writes=[rEe])
                            S.op("dve", lambda e: e.tensor_tensor(dec[hh][:, 0:nch], G3[:, :, 31], X3[:, :, 0], ALU.subtract),
                                 reads=[rG, rX], writes=[Rh[hh]["dec"]])
                            S.op("act", lambda e: e.activation(dec[hh][:, 0:nch], dec[hh][:, 0:nch], AF.Exp), reads=[Rh[hh]["dec"]], writes=[Rh[hh]["dec"]])
                            S.op("dve", lambda e: e.scalar_tensor_tensor(qt[hh][:, 0:Tt], qs[:, 0:Tt], QSCALE, Eq[:, 0:Tt], ALU.mult, ALU.mult),
                                 reads=[rqs, rEq], writes=[Rh[hh]["qt"]])
                            S.op("dve", lambda e: e.tensor_tensor(kt[hh][:, 0:Tt], kk[:, 0:Tt], Ek[:, 0:Tt], ALU.mult), reads=[rkk, rEk], writes=[Rh[hh]["kt"]])
                            kend, rke = kendr.next()
                            S.op("dve", lambda e: e.tensor_tensor(kend[:, 0:Tt], kk[:, 0:Tt], Ee[:, 0:Tt], ALU.mult), reads=[rkk, rEe], writes=[rke])
                            for c8 in range(0, nch, 8):
                                hf = (c8 // 8) % 2
                                for c in range(c8, c8 + 8):
                                    S.op("pe", lambda e: e.transpose(self.pst[0:32, hf * 0 + (c - c8) * 128:(c - c8 + 1) * 128], kend[:, c * 32:(c + 1) * 32],
                                                                     self.identb[:, :]),
                                         reads=[rke, self.Rconst], writes=[self.Rpst])
                                S.op("act", lambda e: e.copy(keT[hh][0:32, c8 * 128:(c8 + 8) * 128], self.pst[0:32, 0:1024]),
                                     reads=[self.Rpst], writes=[Rh[hh]["keT"]])
                        corder = list(range(nch)) if d == 0 else list(range(nch - 1, -1, -1))
                        for c in corder:
                            c0 = c * 32
                            if need_out:
                                pS, RpS = self.ps[ci[0] % 2], self.Rps[ci[0] % 2]
                                ci[0] += 1
                                for hh in range(HB):
                                    S.op("pe", lambda e: e.matmul(pS[0:32, hh * 32:(hh + 1) * 32], kt[hh][:, c0:c0 + 32], qt[hh][:, c0:c0 + 32],
                                                                  start=True, stop=True),
                                         reads=[Rh[hh]["kt"], Rh[hh]["qt"]], writes=[RpS])
                                sc, rsc = scr_.next()
                                S.op("dve", lambda e: e.tensor_tensor(sc[:, :].rearrange("p (h t) -> p h t", t=32),
                                                                      pS[0:32, 0:HB * 32].rearrange("p (h t) -> p h t", t=32),
                                                                      msk[:, d * 32:(d + 1) * 32].rearrange("p (o t) -> p o t", o=1).broadcast_to([32, HB, 32]),
                                                                      ALU.mult),
                                     reads=[RpS, self.Rconst], writes=[rsc])
                                for q4 in range(HB // 4):
                                    pO, RpO = self.ps[2 + q4], self.Rps[2 + q4]
                                    for k in range(4):
                                        hh = q4 * 4 + k
                                        S.op("pe", lambda e: e.matmul(pO[0:32, k * 128:(k + 1) * 128], qt[hh][:, c0:c0 + 32], Sbf[:, hh * 128:(hh + 1) * 128],
                                                                      start=True, stop=False),
                                             reads=[Rh[hh]["qt"], RSb], writes=[RpO])
                                        S.op("pe", lambda e: e.matmul(pO[0:32, k * 128:(k + 1) * 128], sc[:, hh * 32:(hh + 1) * 32],
                                                                      vsb[hh][0:32, c * 128:(c + 1) * 128], start=False, stop=True),
                                             reads=[rsc, Rh[hh]["v"]], writes=[RpO])
                                    ob, rob = obr.next()
                                    S.op("act", lambda e: e.copy(ob[:, :], pO[0:32, 0:512]), reads=[RpO], writes=[rob])
                                    h0 = hb * HB + q4 * 4
                                    S.dma("sp", OD[d][t0 + c0:t0 + c0 + 32, h0 * 128:(h0 + 4) * 128], ob[:, :], reads=[rob], writes=[])
                            for q4 in range(HB // 4):
                                pT, RpT = self.ps[4 + q4], self.Rps[4 + q4]
                                for k in range(4):
                                    hh = q4 * 4 + k
                                    S.op("pe", lambda e: e.matmul(pT[:, k * 128:(k + 1) * 128], keT[hh][0:32, c * 128:(c + 1) * 128],
                                                                  vsb[hh][0:32, c * 128:(c + 1) * 128], start=True, stop=True),
                                         reads=[Rh[hh]["keT"], Rh[hh]["v"]], writes=[RpT])
                                for k in range(4):
                                    hh = q4 * 4 + k
                                    S.op("dve", lambda e: e.scalar_tensor_tensor(Sst[:, hh * 128:(hh + 1) * 128], Sst[:, hh * 128:(hh + 1) * 128],
                                                                                 dec[hh][:, c:c + 1], pT[:, k * 128:(k + 1) * 128], ALU.mult, ALU.add),
                                         reads=[Rh[hh]["S"], Rh[hh]["dec"], RpT], writes=[Rh[hh]["S"]])
                            S.op("act", lambda e: e.copy(Sbf[:, :], Sst[:, :]), reads=[Rh[hh]["S"] for hh in range(HB)], writes=[RSb])
            S.barrier()
        wn_out = "mix_out%d" % l
        with contextlib.ExitStack() as st:
            B = self.alloc_main(st, conv=False)
            hT, gT = B["hT"], B["gT"]
            ofr = Ring(nc, st, "hg_of", 2, [128, 2048], F32)
            obr2 = Ring(nc, st, "hg_ob2", 1, [128, 2048], F32)
            onr = Ring(nc, st, "hg_on", 1, [128, 2048], BF16)
            ssr = Ring(nc, st, "hg_ss", 2, [128, 16], F32)
            for g, (t0, Tt) in enumerate(GROUPS):
                stream = 1 if t0 >= TL else 0
                if stream == 1 and not update_ctx:
                    continue
                S.dma("sp", gT[:, 0:NCH * 512].rearrange("p (c t) -> p c t", t=512)[:, :, 0:Tt],
                      SG[:, t0:t0 + Tt].rearrange("(c p) t -> p c t", p=128), reads=[RSG], writes=[B["RgT"]])
                for tt in range(Tt // 128):
                    tk = t0 + tt * 128
                    for half in range(2):
                        of, rof = ofr.next()
                        ob, rob = obr2.next()
                        S.dma("sp", of[:, :], OD[0][tk:tk + 128, half * 2048:(half + 1) * 2048], reads=[ROD], writes=[rof])
                        S.dma("sp", ob[:, :], OD[1][tk:tk + 128, half * 2048:(half + 1) * 2048], reads=[ROD], writes=[rob])
                        S.op("dve", lambda e: e.tensor_tensor(of[:, :], of[:, :], ob[:, :], ALU.add), reads=[rof, rob], writes=[rof])
                        S.op("dve", lambda e: e.tensor_tensor(ob[:, :], of[:, :], of[:, :], ALU.mult), reads=[rof, rob], writes=[rob])
                        ss, rss = ssr.next()
                        S.op("dve", lambda e: e.tensor_reduce(ss[:, :], ob[:, :].rearrange("p (h v) -> p h v", v=128), AX.X, ALU.add), reads=[rob], writes=[rss])
                        S.op("act", lambda e: e.activation(ss[:, :], ss[:, :], AF.Ln, bias=EPS, scale=1.0 / 128), reads=[rss], writes=[rss])
                        S.op("act", lambda e: e.activation(ss[:, :], ss[:, :], AF.Exp, scale=-0.5), reads=[rss], writes=[rss])
                        on, ron = onr.next()
                        S.op("dve", lambda e: e.tensor_tensor(on[:, :].rearrange("p (h v) -> p h v", v=128), of[:, :].rearrange("p (h v) -> p h v", v=128),
                                                              ss[:, :].rearrange("p (h o) -> p h o", o=1).broadcast_to([128, 16, 128]), ALU.mult),
                             reads=[rof, rss], writes=[ron])
                        for h8 in range(2):
                            for k in range(8):
                                S.op("pe", lambda e: e.transpose(self.pst[:, k * 128:(k + 1) * 128], on[:, (h8 * 8 + k) * 128:(h8 * 8 + k + 1) * 128], self.identb[:, :]),
                                     reads=[ron, self.Rconst], writes=[self.Rpst])
                            for k in range(8):
                                hd = half * 16 + h8 * 8 + k
                                S.op("dve", lambda e: e.scalar_tensor_tensor(hT[:, hd * 512 + tt * 128:hd * 512 + tt * 128 + 128], self.pst[:, k * 128:(k + 1) * 128],
                                                                             gcol, gT[:, hd * 512 + tt * 128:hd * 512 + tt * 128 + 128], ALU.mult, ALU.mult),
                                     reads=[self.Rpst, self.Rcols, B["RgT"]], writes=[B["RhT"]])
                self.gemm(B, wn_out, list(range(16)), NCH, lambda k: hT[:, k * 512:k * 512 + Tt], [B["RhT"]], Tt,
                          self.resid_epi(B, g, 1, Tt, t0, stream))
            self.conv.detach()
            S.barrier()

    def layer(self, l):
        kind = l % 3
        update_ctx = l < DEPTH - 1
        ctx_needed = update_ctx or kind != 1
        if "ada" in self.phases:
            self.ada(l)
        if "ffn_a" in self.phases:
            self.ffn(l, "a", ctx_needed)
        if "mix" in self.phases:
            if kind == 1:
                self.mix_conv(l)
            elif kind == 2:
                self.mix_na(l, update_ctx)
            else:
                self.mix_hgrn(l, update_ctx)
        if "ffn_b" in self.phases:
            self.ffn(l, "b", update_ctx)


def host_cols(P, inputs, b):
    cols = np.zeros((128, P.ncols()), np.float32)

    def col(v):
        return np.ascontiguousarray(np.asarray(v, np.float32).reshape(32, 128).T)
    cols[:, 0:32] = col(inputs["c"][b])
    cols[:, 32:64] = col(inputs["c_ctx"])
    for l in range(DEPTH):
        for j in range(9):
            o = P.col_off("bias", l, j)
            cols[:, o:o + 32] = col(inputs["ada_bias"][l, j * D:(j + 1) * D])
        for j in range(3):
            o = P.col_off("normw", l, j)
            cols[:, o:o + 32] = col(inputs["norm_w"][l, j])
        for d in range(2):
            o = P.col_off("lb", l, d)
            cols[:, o:o + 32] = col(inputs["hgrn_lb"][l, d])
    for j in range(3):
        o = P.col_off("conv", 0, j)
        cols[:, o:o + 32] = col(inputs["sc_conv"][0, j])
    o = P.col_off("misc", 0, 0)
    cols[:, o] = inputs["na_q_norm"][0]
    cols[:, o + 1] = inputs["na_k_norm"][0]
    cols[:, o + 2] = inputs["hgrn_gnorm"][0]
    cols[:, o + 3] = inputs["hgrn_gnorm"][1]
    return cols


def host_weights(P, inputs):
    w = {}
    for l in P.layers:
        kind, jj = l % 3, l // 3
        w["wi_a%d" % l] = inputs["ffn_wi"][l, 0]
        w["wo_a%d" % l] = inputs["ffn_wo"][l, 0]
        w["wi_b%d" % l] = inputs["ffn_wi"][l, 1]
        w["wo_b%d" % l] = inputs["ffn_wo"][l, 1]
        if kind == 0:
            w["mix_in%d" % l] = inputs["hgrn_w_in"][jj]
            w["mix_out%d" % l] = inputs["hgrn_w_out"][jj]
        elif kind == 1:
            w["mix_in%d" % l] = inputs["sc_w_in"][jj]
            w["mix_out%d" % l] = inputs["sc_w_out"][jj]
        else:
            w["mix_in%d" % l] = inputs["na_w_qkv"][jj]
            w["mix_out%d" % l] = inputs["na_w_out"][jj]
        w["ada_down%d" % l] = inputs["ada_down"][l]
        w["ada_up%d" % l] = inputs["ada_up"][l]
    return w


def host_bext(rpb):
    qc = np.arange(64)[:, None]
    kc = np.arange(64)[None, :]
    ws = np.clip(qc - 8, 0, 48)
    valid = (kc >= ws) & (kc < ws + 16)
    bidx = np.clip(kc - qc + 15, 0, 30)
    g = rpb[:, :, bidx]
    g = np.transpose(g, (0, 2, 1, 3))
    out = np.where(valid[None, :, None, :], g, np.float32(-30000.0)).astype(np.float32)
    return np.ascontiguousarray(out.reshape(32, 64, 960))


def make_in_maps(P, inputs, cores):
    inputs = {k: np.asarray(v) for k, v in inputs.items()}
    w = host_weights(P, inputs)
    ident = np.eye(128, dtype=np.float32)
    bext = host_bext(inputs["na_rpb"][0]) if 2 in P.layers else None
    maps = []
    for b in cores:
        xt = np.ascontiguousarray(np.concatenate([inputs["x"][b], inputs["ctx"][b]], axis=0).T)
        m = {"xt_in": xt, "cols": host_cols(P, inputs, b), "ident": ident}
        if 2 in P.layers:
            m["bext"] = bext
        if 0 in P.layers or 3 in P.layers:
            si = np.arange(32)[:, None]
            ti = np.arange(32)[None, :]
            m["hmask"] = np.concatenate([(si <= ti), (si >= ti)], axis=1).astype(np.float32)
        m.update(w)
        maps.append(m)
    return maps


def kernel(**inputs):
    P = Prog()
    nc = P.build()
    maps = make_in_maps(P, inputs, list(range(N_CORES)))
    res = run_bass_kernel_spmd(nc, maps, core_ids=list(range(N_CORES)))
    out = np.stack([np.ascontiguousarray(np.asarray(r["out"])[:, 0:TL].T) for r in res.results], axis=0)
    return out.astype(np.float32)
```
